# Optimizing a Trainium2 kernel written in Bass

```python
import math
import jax, jax.numpy as jnp
from jax import lax
import numpy as np

D_MODEL = 1024
BATCH = 8
SEQ = 2048
DEPTH = 1

D_MIX = D_MODEL
D_A = D_MIX // 2
D_B = D_MIX - D_A
N_GROUPS_A = 8
GROUP_DIM_A = D_A // N_GROUPS_A
HEAD_DIM = 64
N_HEADS_B = D_B // HEAD_DIM
CHUNK = 128
Q_BLOCK = 128
D_IN_PROJ = 3 * D_A + 4 * D_B
LN_EPS = 1e-5
DEEPNORM_ALPHA = (2.0 * DEPTH) ** 0.25
DEEPNORM_BETA = (8.0 * DEPTH) ** -0.25

kernel_name = "hymba_gmlp_stickbreaking_deepnorm_adaln"


def layer_norm(x, g, b):
    xf = x.astype(jnp.float32)
    mu = jnp.mean(xf, axis=-1, keepdims=True)
    var = jnp.mean(jnp.square(xf - mu), axis=-1, keepdims=True)
    y = (xf - mu) * lax.rsqrt(var + LN_EPS) * g.astype(jnp.float32) + b.astype(jnp.float32)
    return y.astype(x.dtype)


def chunked_sgu(u, v, ln_g, ln_b, w_s, b_s):
    bsz, seq, _ = u.shape
    n_chunks = seq // CHUNK
    v = layer_norm(v, ln_g, ln_b)
    v = v.reshape(bsz, n_chunks, CHUNK, N_GROUPS_A, GROUP_DIM_A)
    causal = jnp.tril(jnp.ones((CHUNK, CHUNK), dtype=bool))
    w = jnp.where(causal[None], w_s, 0.0).astype(v.dtype)
    mixed = jnp.einsum('gts,bnsgc->bntgc', w, v) + b_s.T[None, None, :, :, None]
    return u * mixed.reshape(bsz, seq, D_A)


def stick_breaking_attention(q, k, v):
    bsz, n_heads, seq, dh = q.shape
    n_blocks = seq // Q_BLOCK
    scale = 1.0 / math.sqrt(dh)
    q_blocks = q.reshape(bsz, n_heads, n_blocks, Q_BLOCK, dh).transpose(2, 0, 1, 3, 4)
    s_pos = jnp.arange(seq)

    def one_block(args):
        qb, blk = args
        z = jnp.einsum('bhtd,bhsd->bhts', qb, k).astype(jnp.float32) * scale
        t_pos = blk * Q_BLOCK + jnp.arange(Q_BLOCK)
        causal = s_pos[None, :] < t_pos[:, None]
        log_beta = jax.nn.log_sigmoid(z)
        log_1m_beta = jnp.where(causal, -jax.nn.softplus(z), 0.0)
        log_stick = lax.cumsum(log_1m_beta, axis=3, reverse=True) - log_1m_beta
        a = jnp.where(causal, jnp.exp(log_beta + log_stick), 0.0).astype(v.dtype)
        return jnp.einsum('bhts,bhsd->bhtd', a, v)

    out = lax.map(one_block, (q_blocks, jnp.arange(n_blocks)))
    return out.transpose(1, 0, 3, 2, 4).reshape(bsz, seq, n_heads * dh)


def setup_inputs(seed: int = 0) -> dict:
    key = jax.random.key(seed)
    ks = jax.random.split(key, 12)
    f32 = jnp.float32
    x = jax.random.normal(ks[0], (BATCH, SEQ, D_MODEL), f32)
    c = jax.random.normal(ks[1], (BATCH, D_MODEL), f32)
    w_ada = jax.random.normal(ks[2], (DEPTH, D_MODEL, 3 * D_MODEL), f32) * (0.5 * D_MODEL ** -0.5)
    b_ada = jax.random.normal(ks[3], (DEPTH, 3 * D_MODEL), f32) * 0.02
    w_in = jax.random.normal(ks[4], (DEPTH, D_MODEL, D_IN_PROJ), f32) * D_MODEL ** -0.5
    sgu_ln_g = 1.0 + 0.05 * jax.random.normal(ks[5], (DEPTH, D_A), f32)
    sgu_ln_b = 0.02 * jax.random.normal(ks[6], (DEPTH, D_A), f32)
    w_spatial = jax.random.normal(ks[7], (DEPTH, N_GROUPS_A, CHUNK, CHUNK), f32) * CHUNK ** -0.5
    b_spatial = 1.0 + 0.1 * jax.random.normal(ks[8], (DEPTH, N_GROUPS_A, CHUNK), f32)
    w_out = jax.random.normal(ks[9], (DEPTH, D_MIX, D_MODEL), f32) * (DEEPNORM_BETA * D_MIX ** -0.5)
    ln_g = 1.0 + 0.05 * jax.random.normal(ks[10], (DEPTH, D_MODEL), f32)
    ln_b = 0.02 * jax.random.normal(ks[11], (DEPTH, D_MODEL), f32)
    return {"x": x, "c": c, "w_ada": w_ada, "b_ada": b_ada, "w_in": w_in,
            "sgu_ln_g": sgu_ln_g, "sgu_ln_b": sgu_ln_b, "w_spatial": w_spatial,
            "b_spatial": b_spatial, "w_out": w_out, "ln_g": ln_g, "ln_b": ln_b}


def reference(x, c, w_ada, b_ada, w_in, sgu_ln_g, sgu_ln_b, w_spatial, b_spatial, w_out, ln_g, ln_b):
    bsz, seq, _ = x.shape
    split_points = np.cumsum([D_A, D_A, D_A, D_B, D_B, D_B])
    silu_c = jax.nn.silu(c)
    for layer in range(DEPTH):
        mod = silu_c @ w_ada[layer] + b_ada[layer]
        shift, scale, gate = jnp.split(mod, 3, axis=-1)
        h = x * (1.0 + scale[:, None, :]) + shift[:, None, :]

        proj = h @ w_in[layer]
        u_a, v_a, z_a, q, k, v_b, z_b = jnp.split(proj, split_points, axis=-1)

        y_a = chunked_sgu(jax.nn.gelu(u_a), jax.nn.gelu(v_a), sgu_ln_g[layer], sgu_ln_b[layer],
                          w_spatial[layer], b_spatial[layer])

        def heads(t):
            return t.reshape(bsz, seq, N_HEADS_B, HEAD_DIM).transpose(0, 2, 1, 3)
        y_b = stick_breaking_attention(heads(q), heads(k), heads(v_b))

        y = jnp.concatenate([jax.nn.silu(z_a) * y_a, jax.nn.silu(z_b) * y_b], axis=-1)
        y = y @ w_out[layer]

        x = layer_norm(DEEPNORM_ALPHA * x + gate[:, None, :] * y, ln_g[layer], ln_b[layer])
    return x
```

```python
import contextlib
import numpy as np
import concourse.bass as bass
import concourse.mybir as mybir
from concourse.bass_utils import run_bass_kernel_spmd

F32 = mybir.dt.float32
BF16 = mybir.dt.bfloat16
F32R = mybir.dt.float32r
AF = mybir.ActivationFunctionType
ALU = mybir.AluOpType

D = 1024
S = 2048
NT = S // 128
NG = S // 512
DIN = 3584
LN_EPS = 1e-5
ALPHA = 2.0 ** 0.25
ENGS = ["pe", "act", "dve", "pool", "sp"]


class Op:
    __slots__ = ("eng", "fn", "deps", "is_dma", "semkey", "signal", "count", "idx")


class Prog:
    def __init__(self, nc):
        self.nc = nc
        self.eng_ops = {e: [] for e in ENGS}
        self.last_writer = {}
        self.readers = {}
        self.n = 0
        self.pending_fence = {e: [] for e in ENGS}

    def fence(self):
        lastops = [self.eng_ops[e][-1] for e in ENGS if self.eng_ops[e]]
        for e in ENGS:
            self.pending_fence[e] = list(lastops)

    def op(self, eng, fn, reads=(), writes=(), dma=False, semkey=None):
        o = Op()
        o.eng = eng; o.fn = fn; o.is_dma = dma; o.semkey = semkey
        o.signal = False; o.count = None; o.idx = self.n; self.n += 1
        deps = {}
        for b in reads:
            w = self.last_writer.get(b)
            if w is not None:
                deps[w.idx] = w
        for b in writes:
            w = self.last_writer.get(b)
            if w is not None:
                deps[w.idx] = w
            lastr = {}
            for r in self.readers.get(b, ()):
                if r.eng == eng and not r.is_dma and not dma:
                    continue
                if r.is_dma:
                    deps[r.idx] = r
                else:
                    lastr[r.eng] = r
            for r in lastr.values():
                deps[r.idx] = r
        for d in self.pending_fence[eng]:
            deps[d.idx] = d
        self.pending_fence[eng] = []
        if eng == "pe":
            deps = {k: d for k, d in deps.items() if d.eng != "pe"}
        o.deps = list(deps.values())
        for d in o.deps:
            d.signal = True
        for b in reads:
            self.readers.setdefault(b, []).append(o)
        for b in writes:
            self.last_writer[b] = o
            self.readers[b] = []
        self.eng_ops[eng].append(o)
        return o

    def emit(self, final_waits):
        nc = self.nc
        dma_keys = []
        allops = sorted([o for e in ENGS for o in self.eng_ops[e]], key=lambda o: o.idx)
        for o in allops:
            if o.is_dma and o.semkey not in dma_keys:
                dma_keys.append(o.semkey)
        with contextlib.ExitStack() as st:
            esem = {e: st.enter_context(nc.semaphore("s_" + e)) for e in ENGS if e != "sp"}
            dsem = {k: st.enter_context(nc.semaphore("d%d" % i)) for i, k in enumerate(dma_keys)}
            dcount = {k: 0 for k in dma_keys}
            ecount = {e: 0 for e in ENGS}
            for o in allops:
                if o.is_dma:
                    dcount[o.semkey] += 16
                    o.count = dcount[o.semkey]
                elif o.signal:
                    ecount[o.eng] += 1
                    o.count = ecount[o.eng]
            self.stats = dict(ecount=ecount, nops={e: len(self.eng_ops[e]) for e in ENGS}, ndsem=len(dma_keys))
            block = st.enter_context(nc.Block())

            def run_engine(e, engobj):
                waited = {}
                for o in self.eng_ops[e]:
                    for d in o.deps:
                        if d.is_dma:
                            sem = dsem[d.semkey]; key = ("d", d.semkey)
                        else:
                            sem = esem[d.eng]; key = ("e", d.eng)
                        if waited.get(key, 0) >= d.count:
                            continue
                        engobj.wait_ge(sem, d.count)
                        waited[key] = d.count
                    inst = o.fn(engobj)
                    if o.is_dma:
                        inst.then_inc(dsem[o.semkey], 16)
                    elif o.signal:
                        inst.then_inc(esem[e], 1)
                for d in final_waits.get(e, ()):
                    engobj.wait_ge(dsem[d.semkey], d.count)

            @block.tensor
            def _(eng):
                run_engine("pe", eng)

            @block.scalar
            def _(eng):
                run_engine("act", eng)

            @block.vector
            def _(eng):
                run_engine("dve", eng)

            @block.gpsimd
            def _(eng):
                run_engine("pool", eng)

            @block.sync
            def _(eng):
                run_engine("sp", eng)


KB = 1024
SBUF_AVAIL = 212800
NLS = 32
NZF = 6


def build_nc():
    nc = bass.Bass("TRN2", target_bir_lowering=False)

    def din(name, shape):
        return nc.dram_tensor(name, list(shape), F32, kind="ExternalInput").ap()

    x_d = din("x", [S, D])
    ccol_d = din("c_col", [128, 8])
    wada_d = din("w_ada", [D, 3 * D])
    bacol_d = din("b_ada_col", [128, 24])
    bagate_d = din("b_ada_gate", [1, D])
    win_d = din("w_in", [D, DIN])
    wout_d = din("w_out", [D, D])
    sgug_d = din("sgu_g_bc", [128, 512])
    sgub_d = din("sgu_b_bc", [128, 512])
    wsT_d = din("wsT", [128, 8 * 128])
    bsbc_d = din("bs_bc", [128, 4 * 128])
    lng_d = din("lng_bc", [128, D])
    lnb_d = din("lnb_bc", [128, D])
    cst_d = din("consts", [128, 5 * 128])
    y_d = nc.dram_tensor("y", [S, D], F32, kind="ExternalOutput").ap()

    P = Prog(nc)

    class Arena:
        def __init__(self, t, nbytes, dt):
            self.t = t; self.n = nbytes; self.cur = 0; self.dt = dt

        def take(self, shape, dt=F32, off=None):
            esz = 2 if dt == BF16 else 4
            n = 1
            for s_ in shape[1:]:
                n *= s_
            nbytes = n * esz
            if off is None:
                off = self.cur
                self.cur = (off + nbytes + 63) // 64 * 64
            assert off % 4 == 0 and nbytes % 4 == 0 and off + nbytes <= self.n, (off, nbytes, self.n)
            ap = self.t[:, off // 4:(off + nbytes) // 4]
            if dt != self.dt:
                ap = ap.bitcast(dt)
            if len(shape) == 3:
                ap = ap.rearrange("p (a b) -> p a b", a=shape[1])
            return ap

    with contextlib.ExitStack() as st:
        BASE_BYTES = 100 * KB
        base_t = st.enter_context(nc.sbuf_tensor("base", [128, BASE_BYTES // 4], F32))
        banks = [st.enter_context(nc.psum_tensor("bank%d" % i, [128, 512], F32)) for i in range(8)]
        B = Arena(base_t, BASE_BYTES, F32)
        cst = B.take([128, 5, 128])
        identf = cst[:, 0, :]; mask_ts = cst[:, 1, :]; mask_sgu = cst[:, 2, :]
        onesf = B.take([128, 128])
        zerosf = B.take([128, 512])
        zeros512 = zerosf
        cc = B.take([128, 8]); silu_c = B.take([128, 8])
        bac = B.take([128, 24]); modraw = B.take([128, 16]); modT = B.take([128, 16]); scale1p = B.take([128, 8])
        stA = [B.take([128, 12]) for _ in range(4)]
        mvA = [B.take([128, 2]) for _ in range(4)]
        rsA = [B.take([128, 4]) for _ in range(4)]
        mhalf = B.take([128, 1])
        identb = B.take([128, 128], BF16)
        negmask_b = B.take([128, 128], BF16)
        epsc = B.take([128, 1])
        gate_bc = B.take([128, 1024])
        sgug = B.take([128, 512]); sgub = B.take([128, 512])
        wsT_bf = B.take([128, 8, 128], BF16)
        bsbc = B.take([128, 4, 128])
        qT = B.take([128, 4, S], BF16)
        kT = B.take([128, 4, S], BF16)
        vb = B.take([128, NT, 512], BF16)
        YT = B.take([128, 8, S], BF16)
        REG = SBUF_AVAIL - BASE_BYTES - 256
        with nc.sbuf_tensor("RA", [128, REG // 4], F32) as ra_t:
            RA = Arena(ra_t, REG, F32)
            wada_st = [RA.take([128, 8, 512], BF16, off=i * 8 * KB) for i in range(2)]
            silu_bc = RA.take([128, 8, 128], BF16, off=16 * KB)
            wsT_st = RA.take([128, 8, 128], off=18 * KB)
            bagate = gate_bc
            junk = [RA.take([128, 128], off=22 * KB + i * 512) for i in range(2)]
            g32 = [RA.take([128, 512], off=i * 2 * KB) for i in range(3)]
            tmp2 = [RA.take([128, 512], off=6 * KB + i * 2 * KB) for i in range(3)]
            tmp3 = [RA.take([128, 4, 128], off=12 * KB + i * 2 * KB) for i in range(2)]
            van = [RA.take([128, 512], BF16, off=16 * KB + i * KB) for i in range(3)]
            NXT = 8
            xt = [RA.take([128, 1024], off=31 * KB + i * 4 * KB) for i in range(7)]
            xt.append(RA.take([128, 1024], off=18 * KB))
            hT = RA.take([128, 8, S], BF16, off=59 * KB)
            wst = [RA.take([128, 8, 512], BF16, off=o * KB) for o in (91, 23, 99)]
        RE2_BYTES = 40 * KB
        re2_t = st.enter_context(nc.sbuf_tensor("RE2", [128, RE2_BYTES // 4], F32))
        RE2 = Arena(re2_t, RE2_BYTES, F32)
        NA = 6
        Asl = [RE2.take([128, 512], BF16) for i in range(NA)]
        NE = 4
        Esl = [RE2.take([128, 512]) for i in range(NE)]
        wout = RE2.take([128, 8, 1024], BF16)
        lng = RE2.take([128, 1024]); lnb = RE2.take([128, 1024])
        NLR = 6
        RER_WORDS = 2 * 128 + (2 + NLR) * 512
        assert RER_WORDS * 4 + RE2_BYTES <= REG, (RER_WORDS * 4, REG)
        with nc.sbuf_tensor("RER", [128, RER_WORDS], F32R) as rer_t:
            negtri_r = rer_t[:, 0:128]
            negones_r = rer_t[:, 128:256]
            Ssl = [rer_t[:, 256 + i * 512:256 + (i + 1) * 512] for i in range(2)]
            Lsl = [rer_t[:, 256 + (2 + i) * 512:256 + (3 + i) * 512] for i in range(NLR)]
        with nc.sbuf_tensor("RF", [128, 11 * 1024], F32) as rf_t:
            RF = Arena(rf_t, 44 * KB, F32)
            rr = [RF.take([128, 1024]) for i in range(4)]
            xn = [RF.take([128, 1024]) for i in range(2)]
            oo = [RF.take([128, 1024]) for i in range(2)]
            junkF = RF.take([128, 1024])
            xf = [RF.take([128, 1024]) for i in range(2)]

        def dma(q, out, in_, reads, writes, key):
            return P.op(q, lambda e: e.dma_start(out=out, in_=in_), reads=reads, writes=writes, dma=True, semkey=key)

        dma("sp", cst, cst_d.rearrange("p (a b) -> p a b", a=5), [], ["cst"], "cst")
        dma("sp", cc, ccol_d, [], ["cc"], "cc")
        dma("sp", bac, bacol_d, [], ["bac"], "bac")
        dma("sp", bagate[0:1, :], bagate_d, [], ["bagate"], "bagate")
        wada_v = wada_d.rearrange("(kc p) n -> p kc n", p=128)

        def load_wada(cb, after=()):
            slot = cb % 2
            for hf in range(2):
                dma("pool", wada_st[slot][:, hf * 4:(hf + 1) * 4, :], wada_v[:, hf * 4:(hf + 1) * 4, cb * 512:(cb + 1) * 512],
                    list(after), [("wada", slot, hf)], ("wada", slot, hf))

        load_wada(0)
        load_wada(1)
        dma("sp", wsT_st, wsT_d.rearrange("p (a b) -> p a b", a=8), [], ["wsT_st"], "wsT_st")
        dma("sp", bsbc, bsbc_d.rearrange("p (a b) -> p a b", a=4), [], ["bsbc"], "bsbc")
        dma("sp", sgug, sgug_d, [], ["sgug"], "sgug")
        dma("sp", sgub, sgub_d, [], ["sgub"], "sgub")
        win_v = win_d.rearrange("(kc p) n -> p kc n", p=128)
        CB_ORDER = [0, 2, 1, 5, 6, 3, 4]

        def load_wst(n, after=()):
            cb = CB_ORDER[n]
            sl = n % 3
            for hf in range(2):
                dma("pool", wst[sl][:, hf * 4:(hf + 1) * 4, :], win_v[:, hf * 4:(hf + 1) * 4, cb * 512:(cb + 1) * 512],
                    list(after), [("wst", sl, hf)], ("wst", sl, hf))

        P.op("act", lambda e: e.activation(out=silu_c, in_=cc, func=AF.Silu), reads=["cc"], writes=["silu_c"])
        P.op("dve", lambda e: e.memset(onesf, 1.0), writes=["onesf"])
        P.op("dve", lambda e: e.memset(mhalf, -0.5), writes=["mhalf"])
        P.op("dve", lambda e: e.tensor_copy(out=identb, in_=identf), reads=["cst"], writes=["identb"])
        P.op("dve", lambda e: e.tensor_copy(out=negmask_b, in_=mask_ts), reads=["cst"], writes=["negmask_b"])
        P.op("dve", lambda e: e.memset(epsc, LN_EPS), writes=["epsc"])
        P.op("dve", lambda e: e.memset(zerosf, 0.0), writes=["zerosf"])
        for kc in range(8):
            P.op("dve", lambda e, kc=kc: e.tensor_scalar(out=silu_bc[:, kc, :], in0=onesf, scalar1=silu_c[:, kc:kc + 1],
                                                         scalar2=None, op0=ALU.mult),
                 reads=["onesf", "silu_c"], writes=[("silu_bc", kc)])
        for gi in range(8):
            P.op("pool", lambda e, gi=gi: e.tensor_tensor(out=wsT_bf[:, gi, :], in0=wsT_st[:, gi, :], in1=mask_sgu, op=ALU.mult),
                 reads=["wsT_st", "cst"], writes=["wsT_bf"])

        def ada_block(cb):
            slot = cb % 2
            bi = cb % 3
            bk = banks[bi]
            for kc in range(8):
                stp = (kc == 7 and cb < 4)
                P.op("pe", lambda e, slot=slot, kc=kc, bk=bk, stp=stp: e.matmul(
                    bk[:, :], lhsT=silu_bc[:, kc, :], rhs=wada_st[slot][:, kc, :], start=(kc == 0), stop=stp),
                    reads=[("wada", slot, kc // 4), ("silu_bc", kc)], writes=[("pc", bi)])
            if cb < 4:
                for jj in range(4):
                    j = cb * 4 + jj
                    ji = j % 2
                    P.op("dve", lambda e, bk=bk, jj=jj, ji=ji: e.tensor_tensor(out=junk[ji], in0=bk[:, jj * 128:(jj + 1) * 128], in1=identf, op=ALU.mult),
                         reads=[("pc", bi), "cst"], writes=[("junk", ji)])
                    P.op("dve", lambda e, j=j, ji=ji: e.reduce_sum(out=modraw[:, j:j + 1], in_=junk[ji], axis=mybir.AxisListType.X),
                         reads=[("junk", ji)], writes=[("modraw", j)])
            else:
                gb = cb - 4
                P.op("pe", lambda e, gb=gb, bk=bk: e.matmul(
                    bk[:, :], lhsT=onesf[0:1, :], rhs=bagate[0:1, gb * 512:(gb + 1) * 512], start=False, stop=True),
                    reads=["onesf", "bagate"], writes=[("pc", bi)])
                P.op("act", lambda e, gb=gb, bk=bk: e.activation(out=gate_bc[:, gb * 512:(gb + 1) * 512], in_=bk[:, :], func=AF.Identity),
                     reads=[("pc", bi)], writes=[("gate_bc", gb)])
            if cb + 2 < 4:
                load_wada(cb + 2)
            if cb == 3:
                load_wst(0)
            if cb == 0:
                for i in range(NXT):
                    dma("sp", xt[i], x_d[i * 128:(i + 1) * 128, :], [], [("xt", i)] + (["wsT_st"] if i == 7 else []), ("xt", i))

        for cb in range(4):
            ada_block(cb)
        P.op("dve", lambda e: e.tensor_tensor(out=modT, in0=modraw, in1=bac[:, 0:16], op=ALU.add),
             reads=[("modraw", j) for j in range(16)] + ["bac"], writes=["modT"])
        P.op("dve", lambda e: e.tensor_scalar(out=scale1p, in0=modT[:, 8:16], scalar1=1.0, scalar2=None, op0=ALU.add),
             reads=["modT"], writes=["scale1p"])

        trb = [banks[3], banks[4], banks[6], banks[7]]
        trk = [3, 4, 6, 7]
        ev = 0
        pcb = [banks[0], banks[1], banks[2], banks[5]]
        pc = [0]

        def ua_proj(g):
            sl = 0
            for f in range(4):
                bi = pc[0] % 4; pc[0] += 1
                bk = pcb[bi]; bkey = ("pc", bi)
                for kc in range(8):
                    P.op("pe", lambda e, bk=bk, sl=sl, kc=kc, f=f, g=g: e.matmul(
                        bk[:, :], lhsT=wst[sl][:, kc, f * 128:(f + 1) * 128], rhs=hT[:, kc, g * 512:(g + 1) * 512],
                        start=(kc == 0), stop=(kc == 7)),
                        reads=[("wst", sl, kc // 4), ("hT", kc, g)], writes=[bkey])
                tok = slice(g * 512, (g + 1) * 512)
                P.op("act", lambda e, bk=bk, f=f, tok=tok: e.activation(out=YT[:, f, tok], in_=bk[:, :], func=AF.Gelu_apprx_tanh),
                     reads=[bkey], writes=[("YT", f, g)])

        for g in range(NG):
            for kc in range(8):
                b = (g * 8 + kc) % 4
                for tt in range(4):
                    xs = (g * 4 + tt) % NXT
                    P.op("pe", lambda e, b=b, tt=tt, kc=kc, xs=xs: e.transpose(trb[b][:, tt * 128:(tt + 1) * 128],
                                                                                xt[xs][:, kc * 128:(kc + 1) * 128], identf),
                         reads=[("xt", xs), "cst"], writes=[("ps", trk[b])])
                dst = hT[:, kc, g * 512:(g + 1) * 512]
                if ev % 3 == 0:
                    P.op("act", lambda e, b=b, kc=kc, dst=dst: e.activation(out=dst, in_=trb[b][:, :], func=AF.Identity,
                                                                            scale=scale1p[:, kc:kc + 1], bias=modT[:, kc:kc + 1]),
                         reads=[("ps", trk[b]), "scale1p", "modT"], writes=[("hT", kc, g)])
                else:
                    P.op("dve", lambda e, b=b, kc=kc, dst=dst: e.tensor_scalar(out=dst, in0=trb[b][:, :], scalar1=scale1p[:, kc:kc + 1],
                                                                               scalar2=modT[:, kc:kc + 1], op0=ALU.mult, op1=ALU.add),
                         reads=[("ps", trk[b]), "scale1p", "modT"], writes=[("hT", kc, g)])
                ev += 1
            for tt in range(4):
                i = g * 4 + tt + NXT
                if i < NT:
                    xs = i % NXT
                    dma("sp", xt[xs], x_d[i * 128:(i + 1) * 128, :], [], [("xt", xs)], ("xt", xs))
            if g >= 1:
                ua_proj(g - 1)
            if g == 2:
                load_wada(4, after=[("hT", 7, 2)])
                load_wada(5, after=[("hT", 7, 2)])
            if g == 3:
                load_wst(1, after=[("hT", 7, 3)])
        ua_proj(NG - 1)

        sgb = [banks[6], banks[7]]

        def sgu_tail(i):
            ti = i % 3
            g = i // 4
            sb_ = sgb[i % 2]
            for gi in range(8):
                fa, gl = gi // 2, gi % 2
                P.op("pe", lambda e, sb_=sb_, gi=gi, fa=fa, gl=gl, ti=ti: e.matmul(
                    sb_[gl * 64:(gl + 1) * 64, fa * 128:(fa + 1) * 128], lhsT=van[ti][:, gi * 64:(gi + 1) * 64], rhs=wsT_bf[:, gi, :],
                    start=True, stop=True),
                    reads=[("van", ti), "wsT_bf"], writes=[("ps", 6 + i % 2)])
            t3 = i % 2
            P.op("dve", lambda e, sb_=sb_, t3=t3: e.tensor_tensor(out=tmp3[t3], in0=sb_[:, :].rearrange("p (a b) -> p a b", a=4), in1=bsbc, op=ALU.add),
                 reads=[("ps", 6 + i % 2), "bsbc"], writes=[("tmp3", t3)])
            yv = YT[:, 0:4, i * 128:(i + 1) * 128]
            P.op("pool", lambda e, yv=yv, t3=t3: e.tensor_tensor(out=yv, in0=yv, in1=tmp3[t3], op=ALU.mult),
                 reads=[("tmp3", t3)] + [("YT", fa, g) for fa in range(4)], writes=[("YT", fa, g) for fa in range(4)])

        va_bk = {}

        def va_s1(i):
            ti = i % 3
            bk, bkey = va_bk[i]
            P.op("act", lambda e, bk=bk, ti=ti: e.activation(out=g32[ti], in_=bk[:, :], func=AF.Gelu_apprx_tanh),
                 reads=[bkey], writes=[("g32", ti)])
            P.op("dve", lambda e, ti=ti: e.bn_stats(out=stA[ti][:, 0:6], in_=g32[ti]),
                 reads=[("g32", ti)], writes=[("stA", ti)])
            P.op("dve", lambda e, ti=ti: e.bn_aggr(out=mvA[ti], in_=stA[ti][:, 0:6]),
                 reads=[("stA", ti)], writes=[("mvA", ti)])
            P.op("dve", lambda e, ti=ti: e.tensor_scalar(out=rsA[ti][:, 0:1], in0=mvA[ti][:, 1:2], scalar1=LN_EPS, scalar2=None, op0=ALU.add),
                 reads=[("mvA", ti)], writes=[("veps", ti)])
            P.op("pool", lambda e, ti=ti: e.tensor_tensor(out=rsA[ti][:, 1:2], in0=rsA[ti][:, 0:1], in1=mhalf, op=ALU.pow),
                 reads=[("veps", ti), "mhalf"], writes=[("rstd", ti)])

        def va_s2(i):
            ti = i % 3
            P.op("dve", lambda e, ti=ti: e.scalar_tensor_tensor(out=rsA[ti][:, 2:3], in0=mvA[ti][:, 0:1], scalar=-1.0, in1=rsA[ti][:, 1:2],
                                                                 op0=ALU.mult, op1=ALU.mult),
                 reads=[("mvA", ti), ("rstd", ti)], writes=[("nb", ti)])
            P.op("dve", lambda e, ti=ti: e.tensor_scalar(out=tmp2[ti], in0=g32[ti], scalar1=rsA[ti][:, 1:2], scalar2=rsA[ti][:, 2:3],
                                                          op0=ALU.mult, op1=ALU.add),
                 reads=[("g32", ti), ("rstd", ti), ("nb", ti)], writes=[("tmp2", ti)])
            P.op("pool", lambda e, ti=ti: e.tensor_tensor(out=tmp2[ti], in0=tmp2[ti], in1=sgug, op=ALU.mult),
                 reads=[("tmp2", ti), "sgug"], writes=[("tmp2", ti)])

        def va_s3(i):
            ti = i % 3
            P.op("dve", lambda e, ti=ti: e.tensor_tensor(out=van[ti], in0=tmp2[ti], in1=sgub, op=ALU.add),
                 reads=[("tmp2", ti), "sgub"], writes=[("van", ti)])

        def va_s3pe(i):
            ti = i % 3
            sb_ = sgb[i % 2]
            for gi in range(8):
                fa, gl = gi // 2, gi % 2
                P.op("pe", lambda e, sb_=sb_, gi=gi, fa=fa, gl=gl, ti=ti: e.matmul(
                    sb_[gl * 64:(gl + 1) * 64, fa * 128:(fa + 1) * 128], lhsT=van[ti][:, gi * 64:(gi + 1) * 64], rhs=wsT_bf[:, gi, :],
                    start=True, stop=True),
                    reads=[("van", ti), "wsT_bf"], writes=[("ps", 6 + i % 2)])

        def va_s4(i):
            g = i // 4
            sb_ = sgb[i % 2]
            t3 = i % 2
            P.op("dve", lambda e, sb_=sb_, t3=t3: e.tensor_tensor(out=tmp3[t3], in0=sb_[:, :].rearrange("p (a b) -> p a b", a=4), in1=bsbc, op=ALU.add),
                 reads=[("ps", 6 + i % 2), "bsbc"], writes=[("tmp3", t3)])
            yv = YT[:, 0:4, i * 128:(i + 1) * 128]
            P.op("pool", lambda e, yv=yv, t3=t3: e.tensor_tensor(out=yv, in0=yv, in1=tmp3[t3], op=ALU.mult),
                 reads=[("tmp3", t3)] + [("YT", fa, g) for fa in range(4)], writes=[("YT", fa, g) for fa in range(4)])

        va_stages = [va_s1, va_s2, va_s3, va_s4]
        SGU_LAG = 3
        VA_LAGS = ((3, SGU_LAG + 1), (2, 2), (1, 1))

        drain_steps = []
        for n in range(7):
            cb = CB_ORDER[n]
            sl = n % 3
            wkeys = [("wst", sl, 0), ("wst", sl, 1)]
            if n + 2 < 7:
                load_wst(n + 2)
            if cb == 0:
                pass
            elif cb in (0, 2, 3, 4, 6):
                for f in (range(1) if cb in (3, 4) else range(4)):
                    for g in range(NG):
                        bi = pc[0] % 4; pc[0] += 1
                        bk = pcb[bi]; bkey = ("pc", bi)
                        for kc in range(8):
                            P.op("pe", lambda e, bk=bk, sl=sl, kc=kc, f=f, g=g: e.matmul(
                                bk[:, :], lhsT=wst[sl][:, kc, f * 128:(f + 1) * 128], rhs=hT[:, kc, g * 512:(g + 1) * 512],
                                start=(kc == 0), stop=(kc == 7)),
                                reads=[wkeys[kc // 4], ("hT", kc, g)], writes=[bkey])
                        tok = slice(g * 512, (g + 1) * 512)
                        if cb == 0:
                            P.op("act", lambda e, bk=bk, f=f, tok=tok: e.activation(out=YT[:, f, tok], in_=bk[:, :], func=AF.Gelu_apprx_tanh),
                                 reads=[bkey], writes=[("YT", f, g)])
                        elif cb == 2:
                            ti = (f * NG + g) % 3
                            P.op("act", lambda e, bk=bk, ti=ti: e.activation(out=g32[ti], in_=bk[:, :], func=AF.Silu),
                                 reads=[bkey], writes=[("g32", ti)])
                            P.op("dve", lambda e, f=f, tok=tok, ti=ti: e.tensor_tensor(out=YT[:, f, tok], in0=YT[:, f, tok], in1=g32[ti], op=ALU.mult),
                                 reads=[("g32", ti), ("YT", f, g)], writes=[("YT", f, g)])
                        elif cb == 3:
                            P.op("act", lambda e, bk=bk, f=f, tok=tok: e.activation(out=qT[:, f, tok], in_=bk[:, :], func=AF.Identity, scale=0.125),
                                 reads=[bkey], writes=[("qT", f, g)])
                        elif cb == 4:
                            P.op("dve", lambda e, bk=bk, f=f, tok=tok: e.tensor_copy(out=kT[:, f, tok], in_=bk[:, :]),
                                 reads=[bkey], writes=[("kT", f, g)])
                        else:
                            P.op("act", lambda e, bk=bk, f=f, tok=tok: e.activation(out=YT[:, 4 + f, tok], in_=bk[:, :], func=AF.Silu),
                                 reads=[bkey], writes=[("YT", 4 + f, g)])
                        if drain_steps:
                            drain_steps.pop(0)()
            elif cb == 5:
                pass
            else:
                sl_vb = (n + 1) % 3
                wkeys_vb = [("wst", sl_vb, 0), ("wst", sl_vb, 1)]
                for i in range(NT):
                    g = i // 4
                    if cb == 1:
                        for st_no, lag in VA_LAGS:
                            ii = i - lag
                            if 0 <= ii < NT:
                                va_stages[st_no](ii)
                    bi = pc[0] % 4; pc[0] += 1
                    bk = pcb[bi]; bkey = ("pc", bi)
                    for kc in range(8):
                        P.op("pe", lambda e, bk=bk, sl=sl, kc=kc, i=i: e.matmul(
                            bk[:, :], lhsT=hT[:, kc, i * 128:(i + 1) * 128], rhs=wst[sl][:, kc, :],
                            start=(kc == 0), stop=(kc == 7)),
                            reads=[wkeys[kc // 4], ("hT", kc, g)], writes=[bkey])
                    va_bk[i] = (bk, bkey)
                    va_stages[0](i)
                    if i - SGU_LAG >= 0:
                        va_s3pe(i - SGU_LAG)
                    bi2 = pc[0] % 4; pc[0] += 1
                    bk2 = pcb[bi2]; bkey2 = ("pc", bi2)
                    for kc in range(8):
                        P.op("pe", lambda e, bk2=bk2, sl_vb=sl_vb, kc=kc, i=i: e.matmul(
                            bk2[:, :], lhsT=hT[:, kc, i * 128:(i + 1) * 128], rhs=wst[sl_vb][:, kc, :],
                            start=(kc == 0), stop=(kc == 7)),
                            reads=[wkeys_vb[kc // 4], ("hT", kc, g)], writes=[bkey2])
                    P.op("act", lambda e, bk2=bk2, i=i: e.activation(out=vb[:, i, :], in_=bk2[:, :], func=AF.Identity),
                         reads=[bkey2], writes=[("vb", i)])
                if cb == 1:
                    def mk_drain(step):
                        def f_():
                            for st_no, lag in VA_LAGS:
                                ii = step - lag
                                if 0 <= ii < NT:
                                    va_stages[st_no](ii)
                            if 0 <= step - SGU_LAG < NT:
                                va_s3pe(step - SGU_LAG)
                        return f_
                    drain_steps = [mk_drain(step) for step in range(NT, NT + SGU_LAG + 2)]
            if n == 0:
                ada_block(4)
                ada_block(5)
                P.fence()

        while drain_steps:
            drain_steps.pop(0)()
        P.fence()
        P.op("dve", lambda e: e.tensor_copy(out=negtri_r, in_=cst[:, 3, :]), reads=["cst"], writes=["negtri"])
        P.op("dve", lambda e: e.tensor_copy(out=negones_r, in_=cst[:, 4, :]), reads=["cst"], writes=["negones"])
        def late_loads():
            wout_v = wout_d.rearrange("(c p) n -> p c n", p=128)
            for hf in range(2):
                dma("pool", wout[:, hf * 4:(hf + 1) * 4, :], wout_v[:, hf * 4:(hf + 1) * 4, :], [], [("wout", hf)], ("wout", hf))
            dma("sp", lng, lng_d, [], ["lng"], "lng")
            dma("sp", lnb, lnb_d, [], ["lnb"], "lnb")

        def late_fold():
            for c in range(8):
                P.op("pool", lambda e, c=c: e.tensor_tensor(out=wout[:, c, :], in0=wout[:, c, :], in1=gate_bc, op=ALU.mult),
                     reads=[("wout", c // 4), ("gate_bc", 0), ("gate_bc", 1)], writes=[("wout", c // 4)])


        zf = [banks[i] for i in range(NZF)]
        SLQ, SLK = 5 % 3, 6 % 3

        def proj_ops(f):
            ops = []
            for which, sl in (("q", SLQ), ("k", SLK)):
                for g in range(NG):
                    tok = slice(g * 512, (g + 1) * 512)
                    for kc in range(8):
                        ops.append(lambda sl=sl, kc=kc, f=f, g=g: P.op("pe", lambda e: e.matmul(
                            pjb[:, :], lhsT=wst[sl][:, kc, f * 128:(f + 1) * 128], rhs=hT[:, kc, g * 512:(g + 1) * 512],
                            start=(kc == 0), stop=(kc == 7)),
                            reads=[("wst", sl, kc // 4), ("hT", kc, g)], writes=["pjb"]))
                    if which == "q":
                        ops.append(lambda f=f, g=g, tok=tok: P.op("dve", lambda e: e.tensor_scalar(
                            out=qT[:, f, tok], in0=pjb[:, :], scalar1=0.125, scalar2=None, op0=ALU.mult),
                            reads=["pjb"], writes=[("qT", f, g)]))
                    else:
                        ops.append(lambda f=f, g=g, tok=tok: P.op("dve", lambda e: e.tensor_copy(out=kT[:, f, tok], in_=pjb[:, :]),
                                                                 reads=["pjb"], writes=[("kT", f, g)]))
            return ops

        pending = []
        accb = [banks[6], banks[6]]
        pjb = banks[7]
        zc = [0]; ac = [0]; lc = [0]; ec = [0]
        for hp in range(4):
            if hp == 1:
                late_loads()
            if hp == 2:
                late_fold()
            for J in range(NG):
                ai = 0
                if J == 0:
                    while pending:
                        pending.pop(0)()
                    if hp + 1 < 4:
                        pending = proj_ops(hp + 1)
                acc = accb[ai]
                nblk = 4 * J + 4
                tiles = []
                for I in range(nblk - 1, -1, -1):
                    r = I - 4 * J
                    c0 = 128 * r if r > 0 else 0
                    tiles.append((I, r, c0, 512 - c0))
                n = len(tiles)
                for hl in range(2):
                    P.op("pool", lambda e, hl=hl: e.tensor_copy(out=Ssl[hl], in_=zeros512),
                         reads=["zerosf"], writes=[("S", hl)])
                info = {}

                def emit_z(pos):
                    I, r, c0, N = tiles[pos]
                    ks = slice(I * 128, (I + 1) * 128)
                    qs = slice(J * 512 + c0, (J + 1) * 512)
                    for hl in range(2):
                        pr = slice(hl * 64, (hl + 1) * 64)
                        zi = zc[0] % NZF; zc[0] += 1
                        info[(pos, hl)] = dict(zi=zi)
                        P.op("pe", lambda e, zi=zi, N=N, ks=ks, qs=qs, pr=pr, hp=hp: e.matmul(
                            zf[zi][:, 0:N], lhsT=kT[pr, hp, ks], rhs=qT[pr, hp, qs], start=True, stop=True),
                            reads=[("kT", hp, I // 4), ("qT", hp, J)], writes=[("zf", zi)])
                        if r >= 0:
                            P.op("pe", lambda e, zi=zi: e.matmul(
                                zf[zi][:, 0:128], lhsT=identb, rhs=negmask_b, start=False, stop=True, skip_group_check=True),
                                reads=["identb", "negmask_b"], writes=[("zf", zi)])

                def emit_el(pos):
                    I, r, c0, N = tiles[pos]
                    for hl in range(2):
                        d = info[(pos, hl)]
                        zi = d["zi"]
                        ei = ec[0] % NE; ec[0] += 1
                        d["ei"] = ei
                        P.op("act", lambda e, zi=zi, N=N, ei=ei: e.activation(out=Esl[ei][:, 0:N], in_=zf[zi][:, 0:N], func=AF.Exp),
                             reads=[("zf", zi)], writes=[("E", ei)])
                    for hl in range(2):
                        d = info[(pos, hl)]
                        ei = d["ei"]
                        li = lc[0] % NLR; lc[0] += 1
                        d["li"] = li
                        P.op("act", lambda e, N=N, ei=ei, li=li: e.activation(out=Lsl[li][:, 0:N], in_=Esl[ei][:, 0:N], func=AF.Ln, bias=1.0),
                             reads=[("E", ei)], writes=[("L", li)])

                def emit_fin(pos):
                    I, r, c0, N = tiles[pos]
                    top = (pos == 0)
                    for hl in range(2):
                        d = info[(pos, hl)]
                        zi, li = d["zi"], d["li"]
                        P.op("pe", lambda e, zi=zi, N=N, li=li, top=top: e.matmul(
                            zf[zi][:, 0:N], lhsT=negtri_r, rhs=Lsl[li][:, 0:N], start=False, stop=top, skip_group_check=True),
                            reads=[("L", li), "negtri"], writes=[("zf", zi)])
                        if not top:
                            P.op("pe", lambda e, zi=zi, N=N, c0=c0, hl=hl: e.matmul(
                                zf[zi][:, 0:N], lhsT=negones_r, rhs=Ssl[hl][:, c0:512], start=False, stop=True, skip_group_check=True),
                                reads=[("S", hl), "negones"], writes=[("zf", zi)])
                    for hl in range(2):
                        d = info[(pos, hl)]
                        zi, li = d["zi"], d["li"]
                        if pos < n - 1:
                            P.op("dve", lambda e, li=li, hl=hl, c0=c0, N=N: e.tensor_tensor(out=Ssl[hl][:, c0:512], in0=Ssl[hl][:, c0:512], in1=Lsl[li][:, 0:N], op=ALU.add),
                                 reads=[("S", hl), ("L", li)], writes=[("S", hl)])
                        asl = ac[0] % NA; ac[0] += 1
                        d["asl"] = asl
                        P.op("act", lambda e, zi=zi, N=N, asl=asl: e.activation(out=Asl[asl][:, 0:N], in_=zf[zi][:, 0:N], func=AF.Exp),
                             reads=[("zf", zi)], writes=[("A", asl)])

                def emit_av(pos):
                    I, r, c0, N = tiles[pos]
                    top = (pos == 0)
                    for hl in range(2):
                        asl = info[(pos, hl)]["asl"]
                        pr = slice(hl * 64, (hl + 1) * 64)
                        h = 2 * hp + hl
                        P.op("pe", lambda e, pr=pr, c0=c0, N=N, I=I, h=h, asl=asl, top=top, acc=acc: e.matmul(
                            acc[pr, c0:512], lhsT=vb[:, I, h * 64:(h + 1) * 64], rhs=Asl[asl][:, 0:N], start=top, stop=(I == 0), skip_group_check=True),
                            reads=[("vb", I), ("A", asl)], writes=[("acc", ai, hl)])

                emit_z(0)
                for p in range(n + 2):
                    if p < n:
                        emit_el(p)
                    if 0 <= p - 1 < n:
                        emit_fin(p - 1)
                    if p + 1 < n:
                        emit_z(p + 1)
                    if 0 <= p - 2 < n:
                        emit_av(p - 2)
                    for _ in range(2):
                        if pending:
                            pending.pop(0)()
                tok = slice(J * 512, (J + 1) * 512)
                P.op("dve", lambda e, acc=acc, hp=hp, tok=tok: e.tensor_tensor(out=YT[:, 4 + hp, tok], in0=acc[:, :], in1=YT[:, 4 + hp, tok], op=ALU.mult),
                     reads=[("acc", ai, 0), ("acc", ai, 1), ("YT", 4 + hp, J)], writes=[("YT", 4 + hp, J), ("acc", ai, 0), ("acc", ai, 1)])

        P.fence()
        ob = [(banks[0], banks[1]), (banks[2], banks[3]), (banks[4], banks[5])]
        outs = []
        for i in range(2):
            dma("sp", xf[i], x_d[i * 128:(i + 1) * 128, :], [], [("xf", i)], ("xf", i))

        def f_s1(i):
            g = i // 4
            pi = i % 3
            q = i % 4
            xs = i % 2
            for half in range(2):
                bk = ob[pi][half]
                for c in range(8):
                    P.op("pe", lambda e, bk=bk, c=c, i=i, half=half: e.matmul(
                        bk[:, :], lhsT=YT[:, c, i * 128:(i + 1) * 128], rhs=wout[:, c, half * 512:(half + 1) * 512],
                        start=(c == 0), stop=(c == 7)),
                        reads=[("YT", c, g), ("wout", c // 4)], writes=[("ob", pi, half)])
                P.op("dve", lambda e, bk=bk, q=q, half=half, xs=xs: e.scalar_tensor_tensor(
                    out=rr[q][:, half * 512:(half + 1) * 512], in0=xf[xs][:, half * 512:(half + 1) * 512], scalar=ALPHA, in1=bk[:, :],
                    op0=ALU.mult, op1=ALU.add),
                    reads=[("ob", pi, half), ("xf", xs)], writes=[("rr", q, half)])
            if i + 2 < NT:
                dma("sp", xf[xs], x_d[(i + 2) * 128:(i + 3) * 128, :], [], [("xf", xs)], ("xf", xs))

        def f_s1a(i):
            q = i % 4
            P.op("act", lambda e, q=q: e.activation(out=junkF, in_=rr[q], func=AF.Identity, accum_out=banks[6][:, 8 * q:8 * q + 1]),
                 reads=[("rr", q, 0), ("rr", q, 1)], writes=["junkF", ("stF", q, 0)])
            P.op("act", lambda e, q=q: e.activation(out=junkF, in_=rr[q], func=AF.Square, accum_out=banks[7][:, 8 * q:8 * q + 1]),
                 reads=[("rr", q, 0), ("rr", q, 1)], writes=["junkF", ("stF", q, 1)])

        def f_s1b(i):
            pi = i % 4
            P.op("dve", lambda e, pi=pi: e.tensor_scalar(out=mvA[pi][:, 0:1], in0=banks[6][:, 8 * pi:8 * pi + 1], scalar1=1.0 / D, scalar2=None, op0=ALU.mult),
                 reads=[("stF", pi, 0)], writes=[("mvF", pi)])
            P.op("dve", lambda e, pi=pi: e.tensor_tensor(out=stA[pi][:, 3:4], in0=mvA[pi][:, 0:1], in1=mvA[pi][:, 0:1], op=ALU.mult),
                 reads=[("mvF", pi)], writes=[("msqF", pi)])
            P.op("dve", lambda e, pi=pi: e.scalar_tensor_tensor(out=mvA[pi][:, 1:2], in0=banks[7][:, 8 * pi:8 * pi + 1], scalar=1.0 / D, in1=stA[pi][:, 3:4],
                                                                 op0=ALU.mult, op1=ALU.subtract),
                 reads=[("stF", pi, 1), ("msqF", pi)], writes=[("varF", pi)])

        def f_s1c(i):
            pi = i % 4
            P.op("act", lambda e, pi=pi: e.activation(out=rsA[pi][:, 0:1], in_=mvA[pi][:, 1:2], func=AF.Ln, bias=epsc[:, 0:1]),
                 reads=[("varF", pi), "epsc"], writes=[("vepsF", pi)])
            P.op("act", lambda e, pi=pi: e.activation(out=rsA[pi][:, 1:2], in_=rsA[pi][:, 0:1], func=AF.Exp, scale=-0.5),
                 reads=[("vepsF", pi)], writes=[("rstdF", pi)])

        def f_s2(i):
            pi = i % 2
            q = i % 4
            P.op("dve", lambda e, q=q: e.scalar_tensor_tensor(out=rsA[q][:, 2:3], in0=mvA[q][:, 0:1], scalar=-1.0, in1=rsA[q][:, 1:2],
                                                                 op0=ALU.mult, op1=ALU.mult),
                 reads=[("mvF", q), ("rstdF", q)], writes=[("nbF", q)])
            if i % 2 == 0 or i >= NT - 3:
                P.op("dve", lambda e, pi=pi, q=q: e.tensor_scalar(out=xn[pi], in0=rr[q], scalar1=rsA[q][:, 1:2], scalar2=rsA[q][:, 2:3],
                                                                   op0=ALU.mult, op1=ALU.add),
                     reads=[("rr", q, 0), ("rr", q, 1), ("rstdF", q), ("nbF", q)], writes=[("xn", pi)])
            else:
                P.op("act", lambda e, pi=pi, q=q: e.activation(out=xn[pi], in_=rr[q], func=AF.Identity, scale=rsA[q][:, 1:2], bias=rsA[q][:, 2:3]),
                     reads=[("rr", q, 0), ("rr", q, 1), ("rstdF", q), ("nbF", q)], writes=[("xn", pi)])
            P.op("pool" if i < NT - 3 else "dve", lambda e, pi=pi: e.tensor_tensor(out=xn[pi], in0=xn[pi], in1=lng, op=ALU.mult),
                 reads=[("xn", pi), "lng"], writes=[("xn", pi)])

        def f_s3(i):
            pi = i % 2
            P.op("dve", lambda e, pi=pi: e.tensor_tensor(out=oo[pi], in0=xn[pi], in1=lnb, op=ALU.add),
                 reads=[("xn", pi), "lnb"], writes=[("oo", pi)])
            outs.append(dma("sp", y_d[i * 128:(i + 1) * 128, :], oo[pi], [("oo", pi)], [], ("oo", pi)))

        f_stages = [f_s1, f_s1b, f_s2, f_s3]
        NFS = 5
        for step in range(NT + NFS - 1):
            for fn, lag in ((f_s1, 0), (f_s1a, 0), (f_s1b, 1), (f_s1c, 1), (f_s3, 4), (f_s2, 2)):
                ii = step - lag
                if 0 <= ii < NT:
                    fn(ii)

        P.emit(final_waits={"sp": outs})
    return nc, P


_CACHE = {}


def _consts():
    idx = np.arange(128)
    ident = np.eye(128, dtype=np.float32)
    mask_ts = np.where(idx[None, :] > idx[:, None], 0.0, -30000.0).astype(np.float32)
    mask_sgu = (idx[None, :] >= idx[:, None]).astype(np.float32)
    negtri = -(idx[:, None] >= idx[None, :]).astype(np.float32)
    negones = -np.ones((128, 128), np.float32)
    return np.ascontiguousarray(np.concatenate([ident, mask_ts, mask_sgu, negtri, negones], axis=1))


def kernel(x, c, w_ada, b_ada, w_in, sgu_ln_g, sgu_ln_b, w_spatial, b_spatial, w_out, ln_g, ln_b):
    f = np.float32
    x = np.asarray(x, f); c = np.asarray(c, f)
    w_ada = np.ascontiguousarray(np.asarray(w_ada, f)[0]); b_ada = np.asarray(b_ada, f)[0]
    w_in = np.ascontiguousarray(np.asarray(w_in, f)[0]); w_out = np.ascontiguousarray(np.asarray(w_out, f)[0])
    sg = np.asarray(sgu_ln_g, f)[0]; sb = np.asarray(sgu_ln_b, f)[0]
    ws = np.asarray(w_spatial, f)[0]; bs = np.asarray(b_spatial, f)[0]
    lg = np.asarray(ln_g, f)[0]; lb = np.asarray(ln_b, f)[0]
    if "nc" not in _CACHE:
        _CACHE["nc"] = build_nc()[0]
    nc = _CACHE["nc"]
    shared = {
        "w_ada": w_ada,
        "b_ada_col": np.ascontiguousarray(b_ada.reshape(24, 128).T),
        "b_ada_gate": np.ascontiguousarray(b_ada[2048:3072].reshape(1, 1024)),
        "w_in": w_in, "w_out": w_out,
        "sgu_g_bc": np.ascontiguousarray(np.broadcast_to(sg[None, :], (128, 512))),
        "sgu_b_bc": np.ascontiguousarray(np.broadcast_to(sb[None, :], (128, 512))),
        "wsT": np.ascontiguousarray(ws.transpose(2, 0, 1).reshape(128, 8 * 128)),
        "bs_bc": np.ascontiguousarray(np.repeat(bs.reshape(4, 2, 1, 128), 64, axis=2).reshape(4, 128, 128).transpose(1, 0, 2).reshape(128, 512)),
        "lng_bc": np.ascontiguousarray(np.broadcast_to(lg[None, :], (128, 1024))),
        "lnb_bc": np.ascontiguousarray(np.broadcast_to(lb[None, :], (128, 1024))),
        "consts": _consts(),
    }
    in_maps = []
    for b in range(8):
        m = dict(shared)
        m["x"] = np.ascontiguousarray(x[b])
        m["c_col"] = np.ascontiguousarray(c[b].reshape(8, 128).T)
        in_maps.append(m)
    res = run_bass_kernel_spmd(nc, in_maps, core_ids=list(range(8)))
    return np.stack([np.asarray(r["y"], dtype=np.float32) for r in res.results], axis=0)
```

```python
import contextlib
import numpy as np
import concourse.bass as bass
import concourse.mybir as mybir
from concourse.bass_utils import run_bass_kernel_spmd

F32 = mybir.dt.float32
BF16 = mybir.dt.bfloat16
F32R = mybir.dt.float32r
AF = mybir.ActivationFunctionType
ALU = mybir.AluOpType

D = 1024
S = 2048
NT = S // 128
NG = S // 512
DIN = 3584
LN_EPS = 1e-5
ALPHA = 2.0 ** 0.25
ENGS = ["pe", "act", "dve", "pool", "sp"]


class Op:
    __slots__ = ("eng", "fn", "deps", "is_dma", "semkey", "signal", "count", "idx")


class Prog:
    def __init__(self, nc):
        self.nc = nc
        self.eng_ops = {e: [] for e in ENGS}
        self.last_writer = {}
        self.readers = {}
        self.n = 0
        self.pending_fence = {e: [] for e in ENGS}

    def fence(self):
        lastops = [self.eng_ops[e][-1] for e in ENGS if self.eng_ops[e]]
        for e in ENGS:
            self.pending_fence[e] = list(lastops)

    def op(self, eng, fn, reads=(), writes=(), dma=False, semkey=None):
        o = Op()
        o.eng = eng; o.fn = fn; o.is_dma = dma; o.semkey = semkey
        o.signal = False; o.count = None; o.idx = self.n; self.n += 1
        deps = {}
        for b in reads:
            w = self.last_writer.get(b)
            if w is not None:
                deps[w.idx] = w
        for b in writes:
            w = self.last_writer.get(b)
            if w is not None:
                deps[w.idx] = w
            lastr = {}
            for r in self.readers.get(b, ()):
                if r.eng == eng and not r.is_dma and not dma:
                    continue
                if r.is_dma:
                    deps[r.idx] = r
                else:
                    lastr[r.eng] = r
            for r in lastr.values():
                deps[r.idx] = r
        for d in self.pending_fence[eng]:
            deps[d.idx] = d
        self.pending_fence[eng] = []
        if eng == "pe":
            deps = {k: d for k, d in deps.items() if d.eng != "pe"}
        o.deps = list(deps.values())
        for d in o.deps:
            d.signal = True
        for b in reads:
            self.readers.setdefault(b, []).append(o)
        for b in writes:
            self.last_writer[b] = o
            self.readers[b] = []
        self.eng_ops[eng].append(o)
        return o

    def emit(self, final_waits):
        nc = self.nc
        dma_keys = []
        allops = sorted([o for e in ENGS for o in self.eng_ops[e]], key=lambda o: o.idx)
        for o in allops:
            if o.is_dma and o.semkey not in dma_keys:
                dma_keys.append(o.semkey)
        with contextlib.ExitStack() as st:
            esem = {e: st.enter_context(nc.semaphore("s_" + e)) for e in ENGS if e != "sp"}
            dsem = {k: st.enter_context(nc.semaphore("d%d" % i)) for i, k in enumerate(dma_keys)}
            dcount = {k: 0 for k in dma_keys}
            ecount = {e: 0 for e in ENGS}
            for o in allops:
                if o.is_dma:
                    dcount[o.semkey] += 16
                    o.count = dcount[o.semkey]
                elif o.signal:
                    ecount[o.eng] += 1
                    o.count = ecount[o.eng]
            self.stats = dict(ecount=ecount, nops={e: len(self.eng_ops[e]) for e in ENGS}, ndsem=len(dma_keys))
            block = st.enter_context(nc.Block())

            def run_engine(e, engobj):
                waited = {}
                for o in self.eng_ops[e]:
                    for d in o.deps:
                        if d.is_dma:
                            sem = dsem[d.semkey]; key = ("d", d.semkey)
                        else:
                            sem = esem[d.eng]; key = ("e", d.eng)
                        if waited.get(key, 0) >= d.count:
                            continue
                        engobj.wait_ge(sem, d.count)
                        waited[key] = d.count
                    inst = o.fn(engobj)
                    if o.is_dma:
                        inst.then_inc(dsem[o.semkey], 16)
                    elif o.signal:
                        inst.then_inc(esem[e], 1)
                for d in final_waits.get(e, ()):
                    engobj.wait_ge(dsem[d.semkey], d.count)

            @block.tensor
            def _(eng):
                run_engine("pe", eng)

            @block.scalar
            def _(eng):
                run_engine("act", eng)

            @block.vector
            def _(eng):
                run_engine("dve", eng)

            @block.gpsimd
            def _(eng):
                run_engine("pool", eng)

            @block.sync
            def _(eng):
                run_engine("sp", eng)


KB = 1024
SBUF_AVAIL = 212800
NLS = 32
NZF = 6


def build_nc():
    nc = bass.Bass("TRN2", target_bir_lowering=False)

    def din(name, shape):
        return nc.dram_tensor(name, list(shape), F32, kind="ExternalInput").ap()

    x_d = din("x", [S, D])
    ccol_d = din("c_col", [128, 8])
    wada_d = din("w_ada", [D, 3 * D])
    bacol_d = din("b_ada_col", [128, 24])
    bagate_d = din("b_ada_gate", [1, D])
    win_d = din("w_in", [D, DIN])
    wout_d = din("w_out", [D, D])
    sgug_d = din("sgu_g_bc", [128, 512])
    sgub_d = din("sgu_b_bc", [128, 512])
    wsT_d = din("wsT", [128, 8 * 128])
    bsbc_d = din("bs_bc", [128, 4 * 128])
    lng_d = din("lng_bc", [128, D])
    lnb_d = din("lnb_bc", [128, D])
    cst_d = din("consts", [128, 5 * 128])
    y_d = nc.dram_tensor("y", [S, D], F32, kind="ExternalOutput").ap()

    P = Prog(nc)

    class Arena:
        def __init__(self, t, nbytes, dt):
            self.t = t; self.n = nbytes; self.cur = 0; self.dt = dt

        def take(self, shape, dt=F32, off=None):
            esz = 2 if dt == BF16 else 4
            n = 1
            for s_ in shape[1:]:
                n *= s_
            nbytes = n * esz
            if off is None:
                off = self.cur
                self.cur = (off + nbytes + 63) // 64 * 64
            assert off % 4 == 0 and nbytes % 4 == 0 and off + nbytes <= self.n, (off, nbytes, self.n)
            ap = self.t[:, off // 4:(off + nbytes) // 4]
            if dt != self.dt:
                ap = ap.bitcast(dt)
            if len(shape) == 3:
                ap = ap.rearrange("p (a b) -> p a b", a=shape[1])
            return ap

    with contextlib.ExitStack() as st:
        BASE_BYTES = 100 * KB
        base_t = st.enter_context(nc.sbuf_tensor("base", [128, BASE_BYTES // 4], F32))
        banks = [st.enter_context(nc.psum_tensor("bank%d" % i, [128, 512], F32)) for i in range(8)]
        B = Arena(base_t, BASE_BYTES, F32)
        cst = B.take([128, 5, 128])
        identf = cst[:, 0, :]; mask_ts = cst[:, 1, :]; mask_sgu = cst[:, 2, :]
        onesf = B.take([128, 128])
        zerosf = B.take([128, 512])
        zeros512 = zerosf
        cc = B.take([128, 8]); silu_c = B.take([128, 8])
        bac = B.take([128, 24]); modraw = B.take([128, 16]); modT = B.take([128, 16]); scale1p = B.take([128, 8])
        stA = [B.take([128, 12]) for _ in range(4)]
        mvA = [B.take([128, 2]) for _ in range(4)]
        rsA = [B.take([128, 4]) for _ in range(4)]
        mhalf = B.take([128, 1])
        identb = B.take([128, 128], BF16)
        negmask_b = B.take([128, 128], BF16)
        epsc = B.take([128, 1])
        gate_bc = B.take([128, 1024])
        sgug = B.take([128, 512]); sgub = B.take([128, 512])
        wsT_bf = B.take([128, 8, 128], BF16)
        bsbc = B.take([128, 4, 128])
        qT = B.take([128, 4, S], BF16)
        kT = B.take([128, 4, S], BF16)
        vb = B.take([128, NT, 512], BF16)
        YT = B.take([128, 8, S], BF16)
        REG = SBUF_AVAIL - BASE_BYTES - 256
        with nc.sbuf_tensor("RA", [128, REG // 4], F32) as ra_t:
            RA = Arena(ra_t, REG, F32)
            wada_st = [RA.take([128, 8, 512], BF16, off=i * 8 * KB) for i in range(2)]
            silu_bc = RA.take([128, 8, 128], BF16, off=16 * KB)
            wsT_st = RA.take([128, 8, 128], off=18 * KB)
            bagate = gate_bc
            junk = [RA.take([128, 128], off=22 * KB + i * 512) for i in range(2)]
            g32 = [RA.take([128, 512], off=i * 2 * KB) for i in range(3)]
            tmp2 = [RA.take([128, 512], off=6 * KB + i * 2 * KB) for i in range(3)]
            tmp3 = [RA.take([128, 4, 128], off=12 * KB + i * 2 * KB) for i in range(2)]
            van = [RA.take([128, 512], BF16, off=16 * KB + i * KB) for i in range(3)]
            NXT = 8
            xt = [RA.take([128, 1024], off=31 * KB + i * 4 * KB) for i in range(7)]
            xt.append(RA.take([128, 1024], off=18 * KB))
            hT = RA.take([128, 8, S], BF16, off=59 * KB)
            wst = [RA.take([128, 8, 512], BF16, off=o * KB) for o in (91, 23, 99)]
        RE2_BYTES = 40 * KB
        re2_t = st.enter_context(nc.sbuf_tensor("RE2", [128, RE2_BYTES // 4], F32))
        RE2 = Arena(re2_t, RE2_BYTES, F32)
        NA = 6
        Asl = [RE2.take([128, 512], BF16) for i in range(NA)]
        NE = 4
        Esl = [RE2.take([128, 512]) for i in range(NE)]
        wout = RE2.take([128, 8, 1024], BF16)
        lng = RE2.take([128, 1024]); lnb = RE2.take([128, 1024])
        NLR = 6
        RER_WORDS = 2 * 128 + (2 + NLR) * 512
        assert RER_WORDS * 4 + RE2_BYTES <= REG, (RER_WORDS * 4, REG)
        with nc.sbuf_tensor("RER", [128, RER_WORDS], F32R) as rer_t:
            negtri_r = rer_t[:, 0:128]
            negones_r = rer_t[:, 128:256]
            Ssl = [rer_t[:, 256 + i * 512:256 + (i + 1) * 512] for i in range(2)]
            Lsl = [rer_t[:, 256 + (2 + i) * 512:256 + (3 + i) * 512] for i in range(NLR)]
        with nc.sbuf_tensor("RF", [128, 11 * 1024], F32) as rf_t:
            RF = Arena(rf_t, 44 * KB, F32)
            rr = [RF.take([128, 1024]) for i in range(4)]
            xn = [RF.take([128, 1024]) for i in range(2)]
            oo = [RF.take([128, 1024]) for i in range(2)]
            junkF = RF.take([128, 1024])
            xf = [RF.take([128, 1024]) for i in range(2)]

        def dma(q, out, in_, reads, writes, key):
            return P.op(q, lambda e: e.dma_start(out=out, in_=in_), reads=reads, writes=writes, dma=True, semkey=key)

        dma("sp", cst, cst_d.rearrange("p (a b) -> p a b", a=5), [], ["cst"], "cst")
        dma("sp", cc, ccol_d, [], ["cc"], "cc")
        dma("sp", bac, bacol_d, [], ["bac"], "bac")
        dma("sp", bagate[0:1, :], bagate_d, [], ["bagate"], "bagate")
        wada_v = wada_d.rearrange("(kc p) n -> p kc n", p=128)

        def load_wada(cb, after=()):
            slot = cb % 2
            for hf in range(2):
                dma("pool", wada_st[slot][:, hf * 4:(hf + 1) * 4, :], wada_v[:, hf * 4:(hf + 1) * 4, cb * 512:(cb + 1) * 512],
                    list(after), [("wada", slot, hf)], ("wada", slot, hf))

        load_wada(0)
        load_wada(1)
        dma("sp", wsT_st, wsT_d.rearrange("p (a b) -> p a b", a=8), [], ["wsT_st"], "wsT_st")
        dma("sp", bsbc, bsbc_d.rearrange("p (a b) -> p a b", a=4), [], ["bsbc"], "bsbc")
        dma("sp", sgug, sgug_d, [], ["sgug"], "sgug")
        dma("sp", sgub, sgub_d, [], ["sgub"], "sgub")
        win_v = win_d.rearrange("(kc p) n -> p kc n", p=128)
        CB_ORDER = [0, 2, 1, 5, 6, 3, 4]

        def load_wst(n, after=()):
            cb = CB_ORDER[n]
            sl = n % 3
            for hf in range(2):
                dma("pool", wst[sl][:, hf * 4:(hf + 1) * 4, :], win_v[:, hf * 4:(hf + 1) * 4, cb * 512:(cb + 1) * 512],
                    list(after), [("wst", sl, hf)], ("wst", sl, hf))

        P.op("act", lambda e: e.activation(out=silu_c, in_=cc, func=AF.Silu), reads=["cc"], writes=["silu_c"])
        P.op("dve", lambda e: e.memset(onesf, 1.0), writes=["onesf"])
        P.op("dve", lambda e: e.memset(mhalf, -0.5), writes=["mhalf"])
        P.op("dve", lambda e: e.tensor_copy(out=identb, in_=identf), reads=["cst"], writes=["identb"])
        P.op("dve", lambda e: e.tensor_copy(out=negmask_b, in_=mask_ts), reads=["cst"], writes=["negmask_b"])
        P.op("dve", lambda e: e.memset(epsc, LN_EPS), writes=["epsc"])
        P.op("dve", lambda e: e.memset(zerosf, 0.0), writes=["zerosf"])
        for kc in range(8):
            P.op("dve", lambda e, kc=kc: e.tensor_scalar(out=silu_bc[:, kc, :], in0=onesf, scalar1=silu_c[:, kc:kc + 1],
                                                         scalar2=None, op0=ALU.mult),
                 reads=["onesf", "silu_c"], writes=[("silu_bc", kc)])
        for gi in range(8):
            P.op("pool", lambda e, gi=gi: e.tensor_tensor(out=wsT_bf[:, gi, :], in0=wsT_st[:, gi, :], in1=mask_sgu, op=ALU.mult),
                 reads=["wsT_st", "cst"], writes=["wsT_bf"])

        def ada_block(cb):
            slot = cb % 2
            bi = cb % 3
            bk = banks[bi]
            for kc in range(8):
                stp = (kc == 7 and cb < 4)
                P.op("pe", lambda e, slot=slot, kc=kc, bk=bk, stp=stp: e.matmul(
                    bk[:, :], lhsT=silu_bc[:, kc, :], rhs=wada_st[slot][:, kc, :], start=(kc == 0), stop=stp),
                    reads=[("wada", slot, kc // 4), ("silu_bc", kc)], writes=[("pc", bi)])
            if cb < 4:
                for jj in range(4):
                    j = cb * 4 + jj
                    ji = j % 2
                    P.op("dve", lambda e, bk=bk, jj=jj, ji=ji: e.tensor_tensor(out=junk[ji], in0=bk[:, jj * 128:(jj + 1) * 128], in1=identf, op=ALU.mult),
                         reads=[("pc", bi), "cst"], writes=[("junk", ji)])
                    P.op("dve", lambda e, j=j, ji=ji: e.reduce_sum(out=modraw[:, j:j + 1], in_=junk[ji], axis=mybir.AxisListType.X),
                         reads=[("junk", ji)], writes=[("modraw", j)])
            else:
                gb = cb - 4
                P.op("pe", lambda e, gb=gb, bk=bk: e.matmul(
                    bk[:, :], lhsT=onesf[0:1, :], rhs=bagate[0:1, gb * 512:(gb + 1) * 512], start=False, stop=True),
                    reads=["onesf", "bagate"], writes=[("pc", bi)])
                P.op("act", lambda e, gb=gb, bk=bk: e.activation(out=gate_bc[:, gb * 512:(gb + 1) * 512], in_=bk[:, :], func=AF.Identity),
                     reads=[("pc", bi)], writes=[("gate_bc", gb)])
            if cb + 2 < 4:
                load_wada(cb + 2)
            if cb == 3:
                load_wst(0)
            if cb == 0:
                for i in range(NXT):
                    dma("sp", xt[i], x_d[i * 128:(i + 1) * 128, :], [], [("xt", i)] + (["wsT_st"] if i == 7 else []), ("xt", i))

        for cb in range(4):
            ada_block(cb)
        P.op("dve", lambda e: e.tensor_tensor(out=modT, in0=modraw, in1=bac[:, 0:16], op=ALU.add),
             reads=[("modraw", j) for j in range(16)] + ["bac"], writes=["modT"])
        P.op("dve", lambda e: e.tensor_scalar(out=scale1p, in0=modT[:, 8:16], scalar1=1.0, scalar2=None, op0=ALU.add),
             reads=["modT"], writes=["scale1p"])

        trb = [banks[3], banks[4], banks[6], banks[7]]
        trk = [3, 4, 6, 7]
        ev = 0
        pcb = [banks[0], banks[1], banks[2], banks[5]]
        pc = [0]

        def ua_proj(g):
            sl = 0
            for f in range(4):
                bi = pc[0] % 4; pc[0] += 1
                bk = pcb[bi]; bkey = ("pc", bi)
                for kc in range(8):
                    P.op("pe", lambda e, bk=bk, sl=sl, kc=kc, f=f, g=g: e.matmul(
                        bk[:, :], lhsT=wst[sl][:, kc, f * 128:(f + 1) * 128], rhs=hT[:, kc, g * 512:(g + 1) * 512],
                        start=(kc == 0), stop=(kc == 7)),
                        reads=[("wst", sl, kc // 4), ("hT", kc, g)], writes=[bkey])
                tok = slice(g * 512, (g + 1) * 512)
                P.op("act", lambda e, bk=bk, f=f, tok=tok: e.activation(out=YT[:, f, tok], in_=bk[:, :], func=AF.Gelu_apprx_tanh),
                     reads=[bkey], writes=[("YT", f, g)])

        for g in range(NG):
            for kc in range(8):
                b = (g * 8 + kc) % 4
                for tt in range(4):
                    xs = (g * 4 + tt) % NXT
                    P.op("pe", lambda e, b=b, tt=tt, kc=kc, xs=xs: e.transpose(trb[b][:, tt * 128:(tt + 1) * 128],
                                                                                xt[xs][:, kc * 128:(kc + 1) * 128], identf),
                         reads=[("xt", xs), "cst"], writes=[("ps", trk[b])])
                dst = hT[:, kc, g * 512:(g + 1) * 512]
                if ev % 2 == 0:
                    P.op("act", lambda e, b=b, kc=kc, dst=dst: e.activation(out=dst, in_=trb[b][:, :], func=AF.Identity,
                                                                            scale=scale1p[:, kc:kc + 1], bias=modT[:, kc:kc + 1]),
                         reads=[("ps", trk[b]), "scale1p", "modT"], writes=[("hT", kc, g)])
                else:
                    P.op("dve", lambda e, b=b, kc=kc, dst=dst: e.tensor_scalar(out=dst, in0=trb[b][:, :], scalar1=scale1p[:, kc:kc + 1],
                                                                               scalar2=modT[:, kc:kc + 1], op0=ALU.mult, op1=ALU.add),
                         reads=[("ps", trk[b]), "scale1p", "modT"], writes=[("hT", kc, g)])
                ev += 1
            for tt in range(4):
                i = g * 4 + tt + NXT
                if i < NT:
                    xs = i % NXT
                    dma("sp", xt[xs], x_d[i * 128:(i + 1) * 128, :], [], [("xt", xs)], ("xt", xs))
            if g >= 1:
                ua_proj(g - 1)
            if g == 2:
                load_wada(4, after=[("hT", 7, 2)])
                load_wada(5, after=[("hT", 7, 2)])
            if g == 3:
                load_wst(1, after=[("hT", 7, 3)])
        ua_proj(NG - 1)

        sgb = [banks[6], banks[7]]

        def sgu_tail(i):
            ti = i % 3
            g = i // 4
            sb_ = sgb[i % 2]
            for gi in range(8):
                fa, gl = gi // 2, gi % 2
                P.op("pe", lambda e, sb_=sb_, gi=gi, fa=fa, gl=gl, ti=ti: e.matmul(
                    sb_[gl * 64:(gl + 1) * 64, fa * 128:(fa + 1) * 128], lhsT=van[ti][:, gi * 64:(gi + 1) * 64], rhs=wsT_bf[:, gi, :],
                    start=True, stop=True),
                    reads=[("van", ti), "wsT_bf"], writes=[("ps", 6 + i % 2)])
            t3 = i % 2
            P.op("dve", lambda e, sb_=sb_, t3=t3: e.tensor_tensor(out=tmp3[t3], in0=sb_[:, :].rearrange("p (a b) -> p a b", a=4), in1=bsbc, op=ALU.add),
                 reads=[("ps", 6 + i % 2), "bsbc"], writes=[("tmp3", t3)])
            yv = YT[:, 0:4, i * 128:(i + 1) * 128]
            P.op("pool", lambda e, yv=yv, t3=t3: e.tensor_tensor(out=yv, in0=yv, in1=tmp3[t3], op=ALU.mult),
                 reads=[("tmp3", t3)] + [("YT", fa, g) for fa in range(4)], writes=[("YT", fa, g) for fa in range(4)])

        va_bk = {}

        def va_s1(i):
            ti = i % 3
            bk, bkey = va_bk[i]
            P.op("act", lambda e, bk=bk, ti=ti: e.activation(out=g32[ti], in_=bk[:, :], func=AF.Gelu_apprx_tanh),
                 reads=[bkey], writes=[("g32", ti)])
            P.op("dve", lambda e, ti=ti: e.bn_stats(out=stA[ti][:, 0:6], in_=g32[ti]),
                 reads=[("g32", ti)], writes=[("stA", ti)])
            P.op("dve", lambda e, ti=ti: e.bn_aggr(out=mvA[ti], in_=stA[ti][:, 0:6]),
                 reads=[("stA", ti)], writes=[("mvA", ti)])
            P.op("dve", lambda e, ti=ti: e.tensor_scalar(out=rsA[ti][:, 0:1], in0=mvA[ti][:, 1:2], scalar1=LN_EPS, scalar2=None, op0=ALU.add),
                 reads=[("mvA", ti)], writes=[("veps", ti)])
            P.op("pool", lambda e, ti=ti: e.tensor_tensor(out=rsA[ti][:, 1:2], in0=rsA[ti][:, 0:1], in1=mhalf, op=ALU.pow),
                 reads=[("veps", ti), "mhalf"], writes=[("rstd", ti)])

        def va_s2(i):
            ti = i % 3
            P.op("dve", lambda e, ti=ti: e.scalar_tensor_tensor(out=rsA[ti][:, 2:3], in0=mvA[ti][:, 0:1], scalar=-1.0, in1=rsA[ti][:, 1:2],
                                                                 op0=ALU.mult, op1=ALU.mult),
                 reads=[("mvA", ti), ("rstd", ti)], writes=[("nb", ti)])
            P.op("dve", lambda e, ti=ti: e.tensor_scalar(out=tmp2[ti], in0=g32[ti], scalar1=rsA[ti][:, 1:2], scalar2=rsA[ti][:, 2:3],
                                                          op0=ALU.mult, op1=ALU.add),
                 reads=[("g32", ti), ("rstd", ti), ("nb", ti)], writes=[("tmp2", ti)])
            P.op("pool", lambda e, ti=ti: e.tensor_tensor(out=tmp2[ti], in0=tmp2[ti], in1=sgug, op=ALU.mult),
                 reads=[("tmp2", ti), "sgug"], writes=[("tmp2", ti)])

        def va_s3(i):
            ti = i % 3
            P.op("dve", lambda e, ti=ti: e.tensor_tensor(out=van[ti], in0=tmp2[ti], in1=sgub, op=ALU.add),
                 reads=[("tmp2", ti), "sgub"], writes=[("van", ti)])

        def va_s3pe(i):
            ti = i % 3
            sb_ = sgb[i % 2]
            for gi in range(8):
                fa, gl = gi // 2, gi % 2
                P.op("pe", lambda e, sb_=sb_, gi=gi, fa=fa, gl=gl, ti=ti: e.matmul(
                    sb_[gl * 64:(gl + 1) * 64, fa * 128:(fa + 1) * 128], lhsT=van[ti][:, gi * 64:(gi + 1) * 64], rhs=wsT_bf[:, gi, :],
                    start=True, stop=True),
                    reads=[("van", ti), "wsT_bf"], writes=[("ps", 6 + i % 2)])

        def va_s4(i):
            g = i // 4
            sb_ = sgb[i % 2]
            t3 = i % 2
            P.op("dve", lambda e, sb_=sb_, t3=t3: e.tensor_tensor(out=tmp3[t3], in0=sb_[:, :].rearrange("p (a b) -> p a b", a=4), in1=bsbc, op=ALU.add),
                 reads=[("ps", 6 + i % 2), "bsbc"], writes=[("tmp3", t3)])
            yv = YT[:, 0:4, i * 128:(i + 1) * 128]
            P.op("pool", lambda e, yv=yv, t3=t3: e.tensor_tensor(out=yv, in0=yv, in1=tmp3[t3], op=ALU.mult),
                 reads=[("tmp3", t3)] + [("YT", fa, g) for fa in range(4)], writes=[("YT", fa, g) for fa in range(4)])

        va_stages = [va_s1, va_s2, va_s3, va_s4]
        SGU_LAG = 3
        VA_LAGS = ((3, SGU_LAG + 1), (2, 2), (1, 1))

        drain_steps = []
        for n in range(7):
            cb = CB_ORDER[n]
            sl = n % 3
            wkeys = [("wst", sl, 0), ("wst", sl, 1)]
            if n + 2 < 7:
                load_wst(n + 2)
            if cb == 0:
                pass
            elif cb in (0, 2, 3, 4, 6):
                for f in (range(1) if cb in (3, 4) else range(4)):
                    for g in (range(1) if cb in (3, 4) else range(NG)):
                        bi = pc[0] % 4; pc[0] += 1
                        bk = pcb[bi]; bkey = ("pc", bi)
                        for kc in range(8):
                            P.op("pe", lambda e, bk=bk, sl=sl, kc=kc, f=f, g=g: e.matmul(
                                bk[:, :], lhsT=wst[sl][:, kc, f * 128:(f + 1) * 128], rhs=hT[:, kc, g * 512:(g + 1) * 512],
                                start=(kc == 0), stop=(kc == 7)),
                                reads=[wkeys[kc // 4], ("hT", kc, g)], writes=[bkey])
                        tok = slice(g * 512, (g + 1) * 512)
                        if cb == 0:
                            P.op("act", lambda e, bk=bk, f=f, tok=tok: e.activation(out=YT[:, f, tok], in_=bk[:, :], func=AF.Gelu_apprx_tanh),
                                 reads=[bkey], writes=[("YT", f, g)])
                        elif cb == 2:
                            ti = (f * NG + g) % 3
                            P.op("act", lambda e, bk=bk, ti=ti: e.activation(out=g32[ti], in_=bk[:, :], func=AF.Silu),
                                 reads=[bkey], writes=[("g32", ti)])
                            P.op("dve", lambda e, f=f, tok=tok, ti=ti: e.tensor_tensor(out=YT[:, f, tok], in0=YT[:, f, tok], in1=g32[ti], op=ALU.mult),
                                 reads=[("g32", ti), ("YT", f, g)], writes=[("YT", f, g)])
                        elif cb == 3:
                            P.op("act", lambda e, bk=bk, f=f, tok=tok: e.activation(out=qT[:, f, tok], in_=bk[:, :], func=AF.Identity, scale=0.125),
                                 reads=[bkey], writes=[("qT", f, g)])
                        elif cb == 4:
                            P.op("dve", lambda e, bk=bk, f=f, tok=tok: e.tensor_copy(out=kT[:, f, tok], in_=bk[:, :]),
                                 reads=[bkey], writes=[("kT", f, g)])
                        else:
                            P.op("act", lambda e, bk=bk, f=f, tok=tok: e.activation(out=YT[:, 4 + f, tok], in_=bk[:, :], func=AF.Silu),
                                 reads=[bkey], writes=[("YT", 4 + f, g)])
                        if drain_steps:
                            drain_steps.pop(0)()
            elif cb == 5:
                pass
            else:
                sl_vb = (n + 1) % 3
                wkeys_vb = [("wst", sl_vb, 0), ("wst", sl_vb, 1)]
                for i in range(NT):
                    g = i // 4
                    if cb == 1:
                        for st_no, lag in VA_LAGS:
                            ii = i - lag
                            if 0 <= ii < NT:
                                va_stages[st_no](ii)
                    bi = pc[0] % 4; pc[0] += 1
                    bk = pcb[bi]; bkey = ("pc", bi)
                    for kc in range(8):
                        P.op("pe", lambda e, bk=bk, sl=sl, kc=kc, i=i: e.matmul(
                            bk[:, :], lhsT=hT[:, kc, i * 128:(i + 1) * 128], rhs=wst[sl][:, kc, :],
                            start=(kc == 0), stop=(kc == 7)),
                            reads=[wkeys[kc // 4], ("hT", kc, g)], writes=[bkey])
                    va_bk[i] = (bk, bkey)
                    va_stages[0](i)
                    if i - SGU_LAG >= 0:
                        va_s3pe(i - SGU_LAG)
                    bi2 = pc[0] % 4; pc[0] += 1
                    bk2 = pcb[bi2]; bkey2 = ("pc", bi2)
                    for kc in range(8):
                        P.op("pe", lambda e, bk2=bk2, sl_vb=sl_vb, kc=kc, i=i: e.matmul(
                            bk2[:, :], lhsT=hT[:, kc, i * 128:(i + 1) * 128], rhs=wst[sl_vb][:, kc, :],
                            start=(kc == 0), stop=(kc == 7)),
                            reads=[wkeys_vb[kc // 4], ("hT", kc, g)], writes=[bkey2])
                    P.op("act", lambda e, bk2=bk2, i=i: e.activation(out=vb[:, i, :], in_=bk2[:, :], func=AF.Identity),
                         reads=[bkey2], writes=[("vb", i)])
                if cb == 1:
                    def mk_drain(step):
                        def f_():
                            for st_no, lag in VA_LAGS:
                                ii = step - lag
                                if 0 <= ii < NT:
                                    va_stages[st_no](ii)
                            if 0 <= step - SGU_LAG < NT:
                                va_s3pe(step - SGU_LAG)
                        return f_
                    drain_steps = [mk_drain(step) for step in range(NT, NT + SGU_LAG + 2)]
            if n == 0:
                ada_block(4)
                ada_block(5)
                P.fence()

        while drain_steps:
            drain_steps.pop(0)()
        P.fence()
        P.op("dve", lambda e: e.tensor_copy(out=negtri_r, in_=cst[:, 3, :]), reads=["cst"], writes=["negtri"])
        P.op("dve", lambda e: e.tensor_copy(out=negones_r, in_=cst[:, 4, :]), reads=["cst"], writes=["negones"])
        def late_loads():
            wout_v = wout_d.rearrange("(c p) n -> p c n", p=128)
            for hf in range(2):
                dma("pool", wout[:, hf * 4:(hf + 1) * 4, :], wout_v[:, hf * 4:(hf + 1) * 4, :], [], [("wout", hf)], ("wout", hf))
            dma("sp", lng, lng_d, [], ["lng"], "lng")
            dma("sp", lnb, lnb_d, [], ["lnb"], "lnb")

        def late_fold():
            for c in range(8):
                P.op("pool", lambda e, c=c: e.tensor_tensor(out=wout[:, c, :], in0=wout[:, c, :], in1=gate_bc, op=ALU.mult),
                     reads=[("wout", c // 4), ("gate_bc", 0), ("gate_bc", 1)], writes=[("wout", c // 4)])


        zf = [banks[i] for i in range(NZF)]
        SLQ, SLK = 5 % 3, 6 % 3

        def proj_ops(f, groups=None):
            ops = []
            if groups is None:
                groups = range(NG)
            for which, sl in (("q", SLQ), ("k", SLK)):
                for g in groups:
                    tok = slice(g * 512, (g + 1) * 512)
                    for kc in range(8):
                        ops.append(lambda sl=sl, kc=kc, f=f, g=g: P.op("pe", lambda e: e.matmul(
                            pjb[:, :], lhsT=wst[sl][:, kc, f * 128:(f + 1) * 128], rhs=hT[:, kc, g * 512:(g + 1) * 512],
                            start=(kc == 0), stop=(kc == 7)),
                            reads=[("wst", sl, kc // 4), ("hT", kc, g)], writes=["pjb"]))
                    if which == "q":
                        ops.append(lambda f=f, g=g, tok=tok: P.op("dve", lambda e: e.tensor_scalar(
                            out=qT[:, f, tok], in0=pjb[:, :], scalar1=0.125, scalar2=None, op0=ALU.mult),
                            reads=["pjb"], writes=[("qT", f, g)]))
                    else:
                        ops.append(lambda f=f, g=g, tok=tok: P.op("dve", lambda e: e.tensor_copy(out=kT[:, f, tok], in_=pjb[:, :]),
                                                                 reads=["pjb"], writes=[("kT", f, g)]))
            return ops

        pending = []
        npop = [0]
        accb = [banks[6], banks[6]]
        pjb = banks[7]
        zc = [0]; ac = [0]; lc = [0]; ec = [0]
        for hp in range(4):
            if hp == 1:
                late_loads()
            if hp == 2:
                late_fold()
            for J in range(NG):
                ai = 0
                if J == 0:
                    while pending:
                        pending.pop(0)()
                    npop[0] = 0
                    if hp == 0:
                        for g_ in range(1, NG):
                            pending = pending + proj_ops(0, groups=[g_])
                    if hp + 1 < 4:
                        pending = pending + proj_ops(hp + 1)
                elif hp == 0:
                    while npop[0] < 18 * J and pending:
                        pending.pop(0)(); npop[0] += 1
                acc = accb[ai]
                nblk = 4 * J + 4
                tiles = []
                for I in range(nblk - 1, -1, -1):
                    r = I - 4 * J
                    c0 = 128 * r if r > 0 else 0
                    tiles.append((I, r, c0, 512 - c0))
                n = len(tiles)
                for hl in range(2):
                    P.op("pool", lambda e, hl=hl: e.tensor_copy(out=Ssl[hl], in_=zeros512),
                         reads=["zerosf"], writes=[("S", hl)])
                info = {}

                def emit_z(pos):
                    I, r, c0, N = tiles[pos]
                    ks = slice(I * 128, (I + 1) * 128)
                    qs = slice(J * 512 + c0, (J + 1) * 512)
                    for hl in range(2):
                        pr = slice(hl * 64, (hl + 1) * 64)
                        zi = zc[0] % NZF; zc[0] += 1
                        info[(pos, hl)] = dict(zi=zi)
                        P.op("pe", lambda e, zi=zi, N=N, ks=ks, qs=qs, pr=pr, hp=hp: e.matmul(
                            zf[zi][:, 0:N], lhsT=kT[pr, hp, ks], rhs=qT[pr, hp, qs], start=True, stop=True),
                            reads=[("kT", hp, I // 4), ("qT", hp, J)], writes=[("zf", zi)])
                        if r >= 0:
                            P.op("pe", lambda e, zi=zi: e.matmul(
                                zf[zi][:, 0:128], lhsT=identb, rhs=negmask_b, start=False, stop=True, skip_group_check=True),
                                reads=["identb", "negmask_b"], writes=[("zf", zi)])

                def emit_el(pos):
                    I, r, c0, N = tiles[pos]
                    for hl in range(2):
                        d = info[(pos, hl)]
                        zi = d["zi"]
                        ei = ec[0] % NE; ec[0] += 1
                        d["ei"] = ei
                        P.op("act", lambda e, zi=zi, N=N, ei=ei: e.activation(out=Esl[ei][:, 0:N], in_=zf[zi][:, 0:N], func=AF.Exp),
                             reads=[("zf", zi)], writes=[("E", ei)])
                    for hl in range(2):
                        d = info[(pos, hl)]
                        ei = d["ei"]
                        li = lc[0] % NLR; lc[0] += 1
                        d["li"] = li
                        P.op("act", lambda e, N=N, ei=ei, li=li: e.activation(out=Lsl[li][:, 0:N], in_=Esl[ei][:, 0:N], func=AF.Ln, bias=1.0),
                             reads=[("E", ei)], writes=[("L", li)])

                def emit_fin(pos):
                    I, r, c0, N = tiles[pos]
                    top = (pos == 0)
                    for hl in range(2):
                        d = info[(pos, hl)]
                        zi, li = d["zi"], d["li"]
                        P.op("pe", lambda e, zi=zi, N=N, li=li, top=top: e.matmul(
                            zf[zi][:, 0:N], lhsT=negtri_r, rhs=Lsl[li][:, 0:N], start=False, stop=top, skip_group_check=True),
                            reads=[("L", li), "negtri"], writes=[("zf", zi)])
                        if not top:
                            P.op("pe", lambda e, zi=zi, N=N, c0=c0, hl=hl: e.matmul(
                                zf[zi][:, 0:N], lhsT=negones_r, rhs=Ssl[hl][:, c0:512], start=False, stop=True, skip_group_check=True),
                                reads=[("S", hl), "negones"], writes=[("zf", zi)])
                    for hl in range(2):
                        d = info[(pos, hl)]
                        zi, li = d["zi"], d["li"]
                        if pos < n - 1:
                            P.op("dve", lambda e, li=li, hl=hl, c0=c0, N=N: e.tensor_tensor(out=Ssl[hl][:, c0:512], in0=Ssl[hl][:, c0:512], in1=Lsl[li][:, 0:N], op=ALU.add),
                                 reads=[("S", hl), ("L", li)], writes=[("S", hl)])
                        asl = ac[0] % NA; ac[0] += 1
                        d["asl"] = asl
                        P.op("act", lambda e, zi=zi, N=N, asl=asl: e.activation(out=Asl[asl][:, 0:N], in_=zf[zi][:, 0:N], func=AF.Exp),
                             reads=[("zf", zi)], writes=[("A", asl)])

                def emit_av(pos):
                    I, r, c0, N = tiles[pos]
                    top = (pos == 0)
                    for hl in range(2):
                        asl = info[(pos, hl)]["asl"]
                        pr = slice(hl * 64, (hl + 1) * 64)
                        h = 2 * hp + hl
                        P.op("pe", lambda e, pr=pr, c0=c0, N=N, I=I, h=h, asl=asl, top=top, acc=acc: e.matmul(
                            acc[pr, c0:512], lhsT=vb[:, I, h * 64:(h + 1) * 64], rhs=Asl[asl][:, 0:N], start=top, stop=(I == 0), skip_group_check=True),
                            reads=[("vb", I), ("A", asl)], writes=[("acc", ai, hl)])

                emit_z(0)
                for p in range(n + 2):
                    if p < n:
                        emit_el(p)
                    if 0 <= p - 1 < n:
                        emit_fin(p - 1)
                    if p + 1 < n:
                        emit_z(p + 1)
                    if 0 <= p - 2 < n:
                        emit_av(p - 2)
                    for _ in range(3 if hp == 0 else 2):
                        if pending:
                            pending.pop(0)(); npop[0] += 1
                tok = slice(J * 512, (J + 1) * 512)
                P.op("dve", lambda e, acc=acc, hp=hp, tok=tok: e.tensor_tensor(out=YT[:, 4 + hp, tok], in0=acc[:, :], in1=YT[:, 4 + hp, tok], op=ALU.mult),
                     reads=[("acc", ai, 0), ("acc", ai, 1), ("YT", 4 + hp, J)], writes=[("YT", 4 + hp, J), ("acc", ai, 0), ("acc", ai, 1)])

        P.fence()
        ob = [(banks[0], banks[1]), (banks[2], banks[3]), (banks[4], banks[5])]
        outs = []
        for i in range(2):
            dma("sp", xf[i], x_d[i * 128:(i + 1) * 128, :], [], [("xf", i)], ("xf", i))

        def f_s1(i):
            g = i // 4
            pi = i % 3
            q = i % 4
            xs = i % 2
            for half in range(2):
                bk = ob[pi][half]
                for c in range(8):
                    P.op("pe", lambda e, bk=bk, c=c, i=i, half=half: e.matmul(
                        bk[:, :], lhsT=YT[:, c, i * 128:(i + 1) * 128], rhs=wout[:, c, half * 512:(half + 1) * 512],
                        start=(c == 0), stop=(c == 7)),
                        reads=[("YT", c, g), ("wout", c // 4)], writes=[("ob", pi, half)])
                P.op("dve", lambda e, bk=bk, q=q, half=half, xs=xs: e.scalar_tensor_tensor(
                    out=rr[q][:, half * 512:(half + 1) * 512], in0=xf[xs][:, half * 512:(half + 1) * 512], scalar=ALPHA, in1=bk[:, :],
                    op0=ALU.mult, op1=ALU.add),
                    reads=[("ob", pi, half), ("xf", xs)], writes=[("rr", q, half)])
            if i + 2 < NT:
                dma("sp", xf[xs], x_d[(i + 2) * 128:(i + 3) * 128, :], [], [("xf", xs)], ("xf", xs))

        def f_s1a(i):
            q = i % 4
            P.op("act", lambda e, q=q: e.activation(out=junkF, in_=rr[q], func=AF.Identity, accum_out=banks[6][:, 8 * q:8 * q + 1]),
                 reads=[("rr", q, 0), ("rr", q, 1)], writes=["junkF", ("stF", q, 0)])
            P.op("act", lambda e, q=q: e.activation(out=junkF, in_=rr[q], func=AF.Square, accum_out=banks[7][:, 8 * q:8 * q + 1]),
                 reads=[("rr", q, 0), ("rr", q, 1)], writes=["junkF", ("stF", q, 1)])

        def f_s1b(i):
            pi = i % 4
            P.op("dve", lambda e, pi=pi: e.tensor_scalar(out=mvA[pi][:, 0:1], in0=banks[6][:, 8 * pi:8 * pi + 1], scalar1=1.0 / D, scalar2=None, op0=ALU.mult),
                 reads=[("stF", pi, 0)], writes=[("mvF", pi)])
            P.op("dve", lambda e, pi=pi: e.tensor_tensor(out=stA[pi][:, 3:4], in0=mvA[pi][:, 0:1], in1=mvA[pi][:, 0:1], op=ALU.mult),
                 reads=[("mvF", pi)], writes=[("msqF", pi)])
            P.op("dve", lambda e, pi=pi: e.scalar_tensor_tensor(out=mvA[pi][:, 1:2], in0=banks[7][:, 8 * pi:8 * pi + 1], scalar=1.0 / D, in1=stA[pi][:, 3:4],
                                                                 op0=ALU.mult, op1=ALU.subtract),
                 reads=[("stF", pi, 1), ("msqF", pi)], writes=[("varF", pi)])

        def f_s1c(i):
            pi = i % 4
            P.op("act", lambda e, pi=pi: e.activation(out=rsA[pi][:, 0:1], in_=mvA[pi][:, 1:2], func=AF.Ln, bias=epsc[:, 0:1]),
                 reads=[("varF", pi), "epsc"], writes=[("vepsF", pi)])
            P.op("act", lambda e, pi=pi: e.activation(out=rsA[pi][:, 1:2], in_=rsA[pi][:, 0:1], func=AF.Exp, scale=-0.5),
                 reads=[("vepsF", pi)], writes=[("rstdF", pi)])

        def f_s2(i):
            pi = i % 2
            q = i % 4
            P.op("dve", lambda e, q=q: e.scalar_tensor_tensor(out=rsA[q][:, 2:3], in0=mvA[q][:, 0:1], scalar=-1.0, in1=rsA[q][:, 1:2],
                                                                 op0=ALU.mult, op1=ALU.mult),
                 reads=[("mvF", q), ("rstdF", q)], writes=[("nbF", q)])
            if i % 2 == 0 or i >= NT - 3:
                P.op("dve", lambda e, pi=pi, q=q: e.tensor_scalar(out=xn[pi], in0=rr[q], scalar1=rsA[q][:, 1:2], scalar2=rsA[q][:, 2:3],
                                                                   op0=ALU.mult, op1=ALU.add),
                     reads=[("rr", q, 0), ("rr", q, 1), ("rstdF", q), ("nbF", q)], writes=[("xn", pi)])
            else:
                P.op("act", lambda e, pi=pi, q=q: e.activation(out=xn[pi], in_=rr[q], func=AF.Identity, scale=rsA[q][:, 1:2], bias=rsA[q][:, 2:3]),
                     reads=[("rr", q, 0), ("rr", q, 1), ("rstdF", q), ("nbF", q)], writes=[("xn", pi)])
            P.op("pool" if i < NT - 3 else "dve", lambda e, pi=pi: e.tensor_tensor(out=xn[pi], in0=xn[pi], in1=lng, op=ALU.mult),
                 reads=[("xn", pi), "lng"], writes=[("xn", pi)])

        def f_s3(i):
            pi = i % 2
            P.op("dve", lambda e, pi=pi: e.tensor_tensor(out=oo[pi], in0=xn[pi], in1=lnb, op=ALU.add),
                 reads=[("xn", pi), "lnb"], writes=[("oo", pi)])
            outs.append(dma("sp", y_d[i * 128:(i + 1) * 128, :], oo[pi], [("oo", pi)], [], ("oo", pi)))

        f_stages = [f_s1, f_s1b, f_s2, f_s3]
        NFS = 5
        for step in range(NT + NFS - 1):
            for fn, lag in ((f_s1, 0), (f_s1a, 0), (f_s1b, 1), (f_s1c, 1), (f_s3, 4), (f_s2, 2)):
                ii = step - lag
                if 0 <= ii < NT:
                    fn(ii)

        P.emit(final_waits={"sp": outs})
    return nc, P


_CACHE = {}


def _consts():
    idx = np.arange(128)
    ident = np.eye(128, dtype=np.float32)
    mask_ts = np.where(idx[None, :] > idx[:, None], 0.0, -30000.0).astype(np.float32)
    mask_sgu = (idx[None, :] >= idx[:, None]).astype(np.float32)
    negtri = -(idx[:, None] >= idx[None, :]).astype(np.float32)
    negones = -np.ones((128, 128), np.float32)
    return np.ascontiguousarray(np.concatenate([ident, mask_ts, mask_sgu, negtri, negones], axis=1))


def kernel(x, c, w_ada, b_ada, w_in, sgu_ln_g, sgu_ln_b, w_spatial, b_spatial, w_out, ln_g, ln_b):
    f = np.float32
    x = np.asarray(x, f); c = np.asarray(c, f)
    w_ada = np.ascontiguousarray(np.asarray(w_ada, f)[0]); b_ada = np.asarray(b_ada, f)[0]
    w_in = np.ascontiguousarray(np.asarray(w_in, f)[0]); w_out = np.ascontiguousarray(np.asarray(w_out, f)[0])
    sg = np.asarray(sgu_ln_g, f)[0]; sb = np.asarray(sgu_ln_b, f)[0]
    ws = np.asarray(w_spatial, f)[0]; bs = np.asarray(b_spatial, f)[0]
    lg = np.asarray(ln_g, f)[0]; lb = np.asarray(ln_b, f)[0]
    if "nc" not in _CACHE:
        _CACHE["nc"] = build_nc()[0]
    nc = _CACHE["nc"]
    shared = {
        "w_ada": w_ada,
        "b_ada_col": np.ascontiguousarray(b_ada.reshape(24, 128).T),
        "b_ada_gate": np.ascontiguousarray(b_ada[2048:3072].reshape(1, 1024)),
        "w_in": w_in, "w_out": w_out,
        "sgu_g_bc": np.ascontiguousarray(np.broadcast_to(sg[None, :], (128, 512))),
        "sgu_b_bc": np.ascontiguousarray(np.broadcast_to(sb[None, :], (128, 512))),
        "wsT": np.ascontiguousarray(ws.transpose(2, 0, 1).reshape(128, 8 * 128)),
        "bs_bc": np.ascontiguousarray(np.repeat(bs.reshape(4, 2, 1, 128), 64, axis=2).reshape(4, 128, 128).transpose(1, 0, 2).reshape(128, 512)),
        "lng_bc": np.ascontiguousarray(np.broadcast_to(lg[None, :], (128, 1024))),
        "lnb_bc": np.ascontiguousarray(np.broadcast_to(lb[None, :], (128, 1024))),
        "consts": _consts(),
    }
    in_maps = []
    for b in range(8):
        m = dict(shared)
        m["x"] = np.ascontiguousarray(x[b])
        m["c_col"] = np.ascontiguousarray(c[b].reshape(8, 128).T)
        in_maps.append(m)
    res = run_bass_kernel_spmd(nc, in_maps, core_ids=list(range(8)))
    return np.stack([np.asarray(r["y"], dtype=np.float32) for r in res.results], axis=0)
```

```python
import contextlib
import numpy as np
import concourse.bass as bass
import concourse.mybir as mybir
from concourse.bass_utils import run_bass_kernel_spmd

F32 = mybir.dt.float32
BF16 = mybir.dt.bfloat16
F32R = mybir.dt.float32r
AF = mybir.ActivationFunctionType
ALU = mybir.AluOpType

D = 1024
S = 2048
NT = S // 128
NG = S // 512
DIN = 3584
LN_EPS = 1e-5
ALPHA = 2.0 ** 0.25
ENGS = ["pe", "act", "dve", "pool", "sp"]


class Op:
    __slots__ = ("eng", "fn", "deps", "is_dma", "semkey", "signal", "count", "idx")


class Prog:
    def __init__(self, nc):
        self.nc = nc
        self.eng_ops = {e: [] for e in ENGS}
        self.last_writer = {}
        self.readers = {}
        self.n = 0
        self.pending_fence = {e: [] for e in ENGS}

    def fence(self):
        lastops = [self.eng_ops[e][-1] for e in ENGS if self.eng_ops[e]]
        for e in ENGS:
            self.pending_fence[e] = list(lastops)

    def op(self, eng, fn, reads=(), writes=(), dma=False, semkey=None):
        o = Op()
        o.eng = eng; o.fn = fn; o.is_dma = dma; o.semkey = semkey
        o.signal = False; o.count = None; o.idx = self.n; self.n += 1
        deps = {}
        for b in reads:
            w = self.last_writer.get(b)
            if w is not None:
                deps[w.idx] = w
        for b in writes:
            w = self.last_writer.get(b)
            if w is not None:
                deps[w.idx] = w
            lastr = {}
            for r in self.readers.get(b, ()):
                if r.eng == eng and not r.is_dma and not dma:
                    continue
                if r.is_dma:
                    deps[r.idx] = r
                else:
                    lastr[r.eng] = r
            for r in lastr.values():
                deps[r.idx] = r
        for d in self.pending_fence[eng]:
            deps[d.idx] = d
        self.pending_fence[eng] = []
        if eng == "pe":
            deps = {k: d for k, d in deps.items() if d.eng != "pe"}
        o.deps = list(deps.values())
        for d in o.deps:
            d.signal = True
        for b in reads:
            self.readers.setdefault(b, []).append(o)
        for b in writes:
            self.last_writer[b] = o
            self.readers[b] = []
        self.eng_ops[eng].append(o)
        return o

    def emit(self, final_waits):
        nc = self.nc
        dma_keys = []
        allops = sorted([o for e in ENGS for o in self.eng_ops[e]], key=lambda o: o.idx)
        for o in allops:
            if o.is_dma and o.semkey not in dma_keys:
                dma_keys.append(o.semkey)
        with contextlib.ExitStack() as st:
            esem = {e: st.enter_context(nc.semaphore("s_" + e)) for e in ENGS if e != "sp"}
            dsem = {k: st.enter_context(nc.semaphore("d%d" % i)) for i, k in enumerate(dma_keys)}
            dcount = {k: 0 for k in dma_keys}
            ecount = {e: 0 for e in ENGS}
            for o in allops:
                if o.is_dma:
                    dcount[o.semkey] += 16
                    o.count = dcount[o.semkey]
                elif o.signal:
                    ecount[o.eng] += 1
                    o.count = ecount[o.eng]
            self.stats = dict(ecount=ecount, nops={e: len(self.eng_ops[e]) for e in ENGS}, ndsem=len(dma_keys))
            block = st.enter_context(nc.Block())

            def run_engine(e, engobj):
                waited = {}
                for o in self.eng_ops[e]:
                    for d in o.deps:
                        if d.is_dma:
                            sem = dsem[d.semkey]; key = ("d", d.semkey)
                        else:
                            sem = esem[d.eng]; key = ("e", d.eng)
                        if waited.get(key, 0) >= d.count:
                            continue
                        engobj.wait_ge(sem, d.count)
                        waited[key] = d.count
                    inst = o.fn(engobj)
                    if o.is_dma:
                        inst.then_inc(dsem[o.semkey], 16)
                    elif o.signal:
                        inst.then_inc(esem[e], 1)
                for d in final_waits.get(e, ()):
                    engobj.wait_ge(dsem[d.semkey], d.count)

            @block.tensor
            def _(eng):
                run_engine("pe", eng)

            @block.scalar
            def _(eng):
                run_engine("act", eng)

            @block.vector
            def _(eng):
                run_engine("dve", eng)

            @block.gpsimd
            def _(eng):
                run_engine("pool", eng)

            @block.sync
            def _(eng):
                run_engine("sp", eng)


KB = 1024
SBUF_AVAIL = 212800
NLS = 32
NZF = 6


def build_nc():
    nc = bass.Bass("TRN2", target_bir_lowering=False)

    def din(name, shape):
        return nc.dram_tensor(name, list(shape), F32, kind="ExternalInput").ap()

    x_d = din("x", [S, D])
    ccol_d = din("c_col", [128, 8])
    wada_d = din("w_ada", [D, 3 * D])
    bacol_d = din("b_ada_col", [128, 24])
    bagate_d = din("b_ada_gate", [1, D])
    win_d = din("w_in", [D, DIN])
    wout_d = din("w_out", [D, D])
    sgug_d = din("sgu_g_bc", [128, 512])
    sgub_d = din("sgu_b_bc", [128, 512])
    wsT_d = din("wsT", [128, 8 * 128])
    bsbc_d = din("bs_bc", [128, 4 * 128])
    lng_d = din("lng_bc", [128, D])
    lnb_d = din("lnb_bc", [128, D])
    cst_d = din("consts", [128, 5 * 128])
    y_d = nc.dram_tensor("y", [S, D], F32, kind="ExternalOutput").ap()

    P = Prog(nc)

    class Arena:
        def __init__(self, t, nbytes, dt):
            self.t = t; self.n = nbytes; self.cur = 0; self.dt = dt

        def take(self, shape, dt=F32, off=None):
            esz = 2 if dt == BF16 else 4
            n = 1
            for s_ in shape[1:]:
                n *= s_
            nbytes = n * esz
            if off is None:
                off = self.cur
                self.cur = (off + nbytes + 63) // 64 * 64
            assert off % 4 == 0 and nbytes % 4 == 0 and off + nbytes <= self.n, (off, nbytes, self.n)
            ap = self.t[:, off // 4:(off + nbytes) // 4]
            if dt != self.dt:
                ap = ap.bitcast(dt)
            if len(shape) == 3:
                ap = ap.rearrange("p (a b) -> p a b", a=shape[1])
            return ap

    with contextlib.ExitStack() as st:
        BASE_BYTES = 100 * KB
        base_t = st.enter_context(nc.sbuf_tensor("base", [128, BASE_BYTES // 4], F32))
        banks = [st.enter_context(nc.psum_tensor("bank%d" % i, [128, 512], F32)) for i in range(8)]
        B = Arena(base_t, BASE_BYTES, F32)
        cst = B.take([128, 5, 128])
        identf = cst[:, 0, :]; mask_ts = cst[:, 1, :]; mask_sgu = cst[:, 2, :]
        onesf = B.take([128, 128])
        zerosf = B.take([128, 512])
        zeros512 = zerosf
        cc = B.take([128, 8]); silu_c = B.take([128, 8])
        bac = B.take([128, 24]); modraw = B.take([128, 16]); modT = B.take([128, 16]); scale1p = B.take([128, 8])
        stA = [B.take([128, 12]) for _ in range(4)]
        mvA = [B.take([128, 2]) for _ in range(4)]
        rsA = [B.take([128, 4]) for _ in range(4)]
        mhalf = B.take([128, 1])
        identb = B.take([128, 128], BF16)
        negmask_b = B.take([128, 128], BF16)
        epsc = B.take([128, 1])
        gate_bc = B.take([128, 1024])
        sgug = B.take([128, 512]); sgub = B.take([128, 512])
        wsT_bf = B.take([128, 8, 128], BF16)
        bsbc = B.take([128, 4, 128])
        qT = B.take([128, 4, S], BF16)
        kT = B.take([128, 4, S], BF16)
        vb = B.take([128, NT, 512], BF16)
        YT = B.take([128, 8, S], BF16)
        REG = SBUF_AVAIL - BASE_BYTES - 256
        with nc.sbuf_tensor("RA", [128, REG // 4], F32) as ra_t:
            RA = Arena(ra_t, REG, F32)
            wada_st = [RA.take([128, 8, 512], BF16, off=i * 8 * KB) for i in range(2)]
            silu_bc = RA.take([128, 8, 128], BF16, off=16 * KB)
            wsT_st = RA.take([128, 8, 128], off=18 * KB)
            bagate = gate_bc
            junk = [RA.take([128, 128], off=22 * KB + i * 512) for i in range(2)]
            g32 = [RA.take([128, 512], off=i * 2 * KB) for i in range(3)]
            tmp2 = [RA.take([128, 512], off=6 * KB + i * 2 * KB) for i in range(3)]
            tmp3 = [RA.take([128, 4, 128], off=12 * KB + i * 2 * KB) for i in range(2)]
            van = [RA.take([128, 512], BF16, off=16 * KB + i * KB) for i in range(3)]
            NXT = 8
            xt = [RA.take([128, 1024], off=31 * KB + i * 4 * KB) for i in range(7)]
            xt.append(RA.take([128, 1024], off=18 * KB))
            hT = RA.take([128, 8, S], BF16, off=59 * KB)
            wst = [RA.take([128, 8, 512], BF16, off=o * KB) for o in (91, 23, 99)]
        RE2_BYTES = 40 * KB
        re2_t = st.enter_context(nc.sbuf_tensor("RE2", [128, RE2_BYTES // 4], F32))
        RE2 = Arena(re2_t, RE2_BYTES, F32)
        NA = 6
        Asl = [RE2.take([128, 512], BF16) for i in range(NA)]
        NE = 4
        Esl = [RE2.take([128, 512]) for i in range(NE)]
        wout = RE2.take([128, 8, 1024], BF16)
        lng = RE2.take([128, 1024]); lnb = RE2.take([128, 1024])
        NLR = 6
        RER_WORDS = 2 * 128 + (2 + NLR) * 512
        assert RER_WORDS * 4 + RE2_BYTES <= REG, (RER_WORDS * 4, REG)
        with nc.sbuf_tensor("RER", [128, RER_WORDS], F32R) as rer_t:
            negtri_r = rer_t[:, 0:128]
            negones_r = rer_t[:, 128:256]
            Ssl = [rer_t[:, 256 + i * 512:256 + (i + 1) * 512] for i in range(2)]
            Lsl = [rer_t[:, 256 + (2 + i) * 512:256 + (3 + i) * 512] for i in range(NLR)]
        with nc.sbuf_tensor("RF", [128, 11 * 1024], F32) as rf_t:
            RF = Arena(rf_t, 44 * KB, F32)
            rr = [RF.take([128, 1024]) for i in range(4)]
            xn = [RF.take([128, 1024]) for i in range(2)]
            oo = [RF.take([128, 1024]) for i in range(2)]
            junkF = RF.take([128, 1024])
            xf = [RF.take([128, 1024]) for i in range(2)]

        def dma(q, out, in_, reads, writes, key):
            return P.op(q, lambda e: e.dma_start(out=out, in_=in_), reads=reads, writes=writes, dma=True, semkey=key)

        dma("sp", cst, cst_d.rearrange("p (a b) -> p a b", a=5), [], ["cst"], "cst")
        dma("sp", cc, ccol_d, [], ["cc"], "cc")
        dma("sp", bac, bacol_d, [], ["bac"], "bac")
        dma("sp", bagate[0:1, :], bagate_d, [], ["bagate"], "bagate")
        wada_v = wada_d.rearrange("(kc p) n -> p kc n", p=128)

        def load_wada(cb, after=()):
            slot = cb % 2
            for hf in range(2):
                dma("pool", wada_st[slot][:, hf * 4:(hf + 1) * 4, :], wada_v[:, hf * 4:(hf + 1) * 4, cb * 512:(cb + 1) * 512],
                    list(after), [("wada", slot, hf)], ("wada", slot, hf))

        load_wada(0)
        load_wada(1)
        dma("sp", wsT_st, wsT_d.rearrange("p (a b) -> p a b", a=8), [], ["wsT_st"], "wsT_st")
        dma("sp", bsbc, bsbc_d.rearrange("p (a b) -> p a b", a=4), [], ["bsbc"], "bsbc")
        dma("sp", sgug, sgug_d, [], ["sgug"], "sgug")
        dma("sp", sgub, sgub_d, [], ["sgub"], "sgub")
        win_v = win_d.rearrange("(kc p) n -> p kc n", p=128)
        CB_ORDER = [0, 2, 1, 5, 6, 3, 4]

        def load_wst(n, after=()):
            cb = CB_ORDER[n]
            sl = n % 3
            for hf in range(2):
                dma("pool", wst[sl][:, hf * 4:(hf + 1) * 4, :], win_v[:, hf * 4:(hf + 1) * 4, cb * 512:(cb + 1) * 512],
                    list(after), [("wst", sl, hf)], ("wst", sl, hf))

        P.op("act", lambda e: e.activation(out=silu_c, in_=cc, func=AF.Silu), reads=["cc"], writes=["silu_c"])
        P.op("dve", lambda e: e.memset(onesf, 1.0), writes=["onesf"])
        P.op("dve", lambda e: e.memset(mhalf, -0.5), writes=["mhalf"])
        P.op("dve", lambda e: e.tensor_copy(out=identb, in_=identf), reads=["cst"], writes=["identb"])
        P.op("dve", lambda e: e.tensor_copy(out=negmask_b, in_=mask_ts), reads=["cst"], writes=["negmask_b"])
        P.op("dve", lambda e: e.memset(epsc, LN_EPS), writes=["epsc"])
        P.op("dve", lambda e: e.memset(zerosf, 0.0), writes=["zerosf"])
        for kc in range(8):
            P.op("dve", lambda e, kc=kc: e.tensor_scalar(out=silu_bc[:, kc, :], in0=onesf, scalar1=silu_c[:, kc:kc + 1],
                                                         scalar2=None, op0=ALU.mult),
                 reads=["onesf", "silu_c"], writes=[("silu_bc", kc)])
        for gi in range(8):
            P.op("pool", lambda e, gi=gi: e.tensor_tensor(out=wsT_bf[:, gi, :], in0=wsT_st[:, gi, :], in1=mask_sgu, op=ALU.mult),
                 reads=["wsT_st", "cst"], writes=["wsT_bf"])

        def ada_block(cb):
            slot = cb % 2
            bi = cb % 3
            bk = banks[bi]
            for kc in range(8):
                stp = (kc == 7 and cb < 4)
                P.op("pe", lambda e, slot=slot, kc=kc, bk=bk, stp=stp: e.matmul(
                    bk[:, :], lhsT=silu_bc[:, kc, :], rhs=wada_st[slot][:, kc, :], start=(kc == 0), stop=stp),
                    reads=[("wada", slot, kc // 4), ("silu_bc", kc)], writes=[("pc", bi)])
            if cb < 4:
                for jj in range(4):
                    j = cb * 4 + jj
                    ji = j % 2
                    P.op("dve", lambda e, bk=bk, jj=jj, ji=ji: e.tensor_tensor(out=junk[ji], in0=bk[:, jj * 128:(jj + 1) * 128], in1=identf, op=ALU.mult),
                         reads=[("pc", bi), "cst"], writes=[("junk", ji)])
                    P.op("dve", lambda e, j=j, ji=ji: e.reduce_sum(out=modraw[:, j:j + 1], in_=junk[ji], axis=mybir.AxisListType.X),
                         reads=[("junk", ji)], writes=[("modraw", j)])
            else:
                gb = cb - 4
                P.op("pe", lambda e, gb=gb, bk=bk: e.matmul(
                    bk[:, :], lhsT=onesf[0:1, :], rhs=bagate[0:1, gb * 512:(gb + 1) * 512], start=False, stop=True),
                    reads=["onesf", "bagate"], writes=[("pc", bi)])
                P.op("act", lambda e, gb=gb, bk=bk: e.activation(out=gate_bc[:, gb * 512:(gb + 1) * 512], in_=bk[:, :], func=AF.Identity),
                     reads=[("pc", bi)], writes=[("gate_bc", gb)])
            if cb + 2 < 4:
                load_wada(cb + 2)
            if cb == 3:
                load_wst(0)
            if cb == 0:
                for i in range(NXT):
                    dma("sp", xt[i], x_d[i * 128:(i + 1) * 128, :], [], [("xt", i)] + (["wsT_st"] if i == 7 else []), ("xt", i))

        for cb in range(4):
            ada_block(cb)
        P.op("dve", lambda e: e.tensor_tensor(out=modT, in0=modraw, in1=bac[:, 0:16], op=ALU.add),
             reads=[("modraw", j) for j in range(16)] + ["bac"], writes=["modT"])
        P.op("dve", lambda e: e.tensor_scalar(out=scale1p, in0=modT[:, 8:16], scalar1=1.0, scalar2=None, op0=ALU.add),
             reads=["modT"], writes=["scale1p"])

        trb = [banks[3], banks[4], banks[6], banks[7]]
        trk = [3, 4, 6, 7]
        ev = 0
        pcb = [banks[0], banks[1], banks[2], banks[5]]
        pc = [0]

        def ua_proj(g):
            sl = 0
            for f in range(4):
                bi = pc[0] % 4; pc[0] += 1
                bk = pcb[bi]; bkey = ("pc", bi)
                for kc in range(8):
                    P.op("pe", lambda e, bk=bk, sl=sl, kc=kc, f=f, g=g: e.matmul(
                        bk[:, :], lhsT=wst[sl][:, kc, f * 128:(f + 1) * 128], rhs=hT[:, kc, g * 512:(g + 1) * 512],
                        start=(kc == 0), stop=(kc == 7)),
                        reads=[("wst", sl, kc // 4), ("hT", kc, g)], writes=[bkey])
                tok = slice(g * 512, (g + 1) * 512)
                P.op("act", lambda e, bk=bk, f=f, tok=tok: e.activation(out=YT[:, f, tok], in_=bk[:, :], func=AF.Gelu_apprx_tanh),
                     reads=[bkey], writes=[("YT", f, g)])

        for g in range(NG):
            for kc in range(8):
                b = (g * 8 + kc) % 4
                for tt in range(4):
                    xs = (g * 4 + tt) % NXT
                    P.op("pe", lambda e, b=b, tt=tt, kc=kc, xs=xs: e.transpose(trb[b][:, tt * 128:(tt + 1) * 128],
                                                                                xt[xs][:, kc * 128:(kc + 1) * 128], identf),
                         reads=[("xt", xs), "cst"], writes=[("ps", trk[b])])
                dst = hT[:, kc, g * 512:(g + 1) * 512]
                if ev % 2 == 0:
                    P.op("act", lambda e, b=b, kc=kc, dst=dst: e.activation(out=dst, in_=trb[b][:, :], func=AF.Identity,
                                                                            scale=scale1p[:, kc:kc + 1], bias=modT[:, kc:kc + 1]),
                         reads=[("ps", trk[b]), "scale1p", "modT"], writes=[("hT", kc, g)])
                else:
                    P.op("dve", lambda e, b=b, kc=kc, dst=dst: e.tensor_scalar(out=dst, in0=trb[b][:, :], scalar1=scale1p[:, kc:kc + 1],
                                                                               scalar2=modT[:, kc:kc + 1], op0=ALU.mult, op1=ALU.add),
                         reads=[("ps", trk[b]), "scale1p", "modT"], writes=[("hT", kc, g)])
                ev += 1
            for tt in range(4):
                i = g * 4 + tt + NXT
                if i < NT:
                    xs = i % NXT
                    dma("sp", xt[xs], x_d[i * 128:(i + 1) * 128, :], [], [("xt", xs)], ("xt", xs))
            if g >= 1:
                ua_proj(g - 1)
            if g == 2:
                load_wada(4, after=[("hT", 7, 2)])
                load_wada(5, after=[("hT", 7, 2)])
            if g == 3:
                load_wst(1, after=[("hT", 7, 3)])
        ua_proj(NG - 1)

        sgb = [banks[6], banks[7]]

        def sgu_tail(i):
            ti = i % 3
            g = i // 4
            sb_ = sgb[i % 2]
            for gi in range(8):
                fa, gl = gi // 2, gi % 2
                P.op("pe", lambda e, sb_=sb_, gi=gi, fa=fa, gl=gl, ti=ti: e.matmul(
                    sb_[gl * 64:(gl + 1) * 64, fa * 128:(fa + 1) * 128], lhsT=van[ti][:, gi * 64:(gi + 1) * 64], rhs=wsT_bf[:, gi, :],
                    start=True, stop=True),
                    reads=[("van", ti), "wsT_bf"], writes=[("ps", 6 + i % 2)])
            t3 = i % 2
            P.op("dve", lambda e, sb_=sb_, t3=t3: e.tensor_tensor(out=tmp3[t3], in0=sb_[:, :].rearrange("p (a b) -> p a b", a=4), in1=bsbc, op=ALU.add),
                 reads=[("ps", 6 + i % 2), "bsbc"], writes=[("tmp3", t3)])
            yv = YT[:, 0:4, i * 128:(i + 1) * 128]
            P.op("pool", lambda e, yv=yv, t3=t3: e.tensor_tensor(out=yv, in0=yv, in1=tmp3[t3], op=ALU.mult),
                 reads=[("tmp3", t3)] + [("YT", fa, g) for fa in range(4)], writes=[("YT", fa, g) for fa in range(4)])

        va_bk = {}

        def va_s1(i):
            ti = i % 3
            bk, bkey = va_bk[i]
            P.op("act", lambda e, bk=bk, ti=ti: e.activation(out=g32[ti], in_=bk[:, :], func=AF.Gelu_apprx_tanh),
                 reads=[bkey], writes=[("g32", ti)])
            P.op("dve", lambda e, ti=ti: e.bn_stats(out=stA[ti][:, 0:6], in_=g32[ti]),
                 reads=[("g32", ti)], writes=[("stA", ti)])
            P.op("dve", lambda e, ti=ti: e.bn_aggr(out=mvA[ti], in_=stA[ti][:, 0:6]),
                 reads=[("stA", ti)], writes=[("mvA", ti)])
            P.op("dve", lambda e, ti=ti: e.tensor_scalar(out=rsA[ti][:, 0:1], in0=mvA[ti][:, 1:2], scalar1=LN_EPS, scalar2=None, op0=ALU.add),
                 reads=[("mvA", ti)], writes=[("veps", ti)])
            P.op("pool", lambda e, ti=ti: e.tensor_tensor(out=rsA[ti][:, 1:2], in0=rsA[ti][:, 0:1], in1=mhalf, op=ALU.pow),
                 reads=[("veps", ti), "mhalf"], writes=[("rstd", ti)])

        def va_s2(i):
            ti = i % 3
            P.op("dve", lambda e, ti=ti: e.scalar_tensor_tensor(out=rsA[ti][:, 2:3], in0=mvA[ti][:, 0:1], scalar=-1.0, in1=rsA[ti][:, 1:2],
                                                                 op0=ALU.mult, op1=ALU.mult),
                 reads=[("mvA", ti), ("rstd", ti)], writes=[("nb", ti)])
            P.op("dve", lambda e, ti=ti: e.tensor_scalar(out=tmp2[ti], in0=g32[ti], scalar1=rsA[ti][:, 1:2], scalar2=rsA[ti][:, 2:3],
                                                          op0=ALU.mult, op1=ALU.add),
                 reads=[("g32", ti), ("rstd", ti), ("nb", ti)], writes=[("tmp2", ti)])
            P.op("pool", lambda e, ti=ti: e.tensor_tensor(out=tmp2[ti], in0=tmp2[ti], in1=sgug, op=ALU.mult),
                 reads=[("tmp2", ti), "sgug"], writes=[("tmp2", ti)])

        def va_s3(i):
            ti = i % 3
            P.op("dve", lambda e, ti=ti: e.tensor_tensor(out=van[ti], in0=tmp2[ti], in1=sgub, op=ALU.add),
                 reads=[("tmp2", ti), "sgub"], writes=[("van", ti)])

        def va_s3pe(i):
            ti = i % 3
            sb_ = sgb[i % 2]
            for gi in range(8):
                fa, gl = gi // 2, gi % 2
                P.op("pe", lambda e, sb_=sb_, gi=gi, fa=fa, gl=gl, ti=ti: e.matmul(
                    sb_[gl * 64:(gl + 1) * 64, fa * 128:(fa + 1) * 128], lhsT=van[ti][:, gi * 64:(gi + 1) * 64], rhs=wsT_bf[:, gi, :],
                    start=True, stop=True),
                    reads=[("van", ti), "wsT_bf"], writes=[("ps", 6 + i % 2)])

        def va_s4(i):
            g = i // 4
            sb_ = sgb[i % 2]
            t3 = i % 2
            P.op("dve", lambda e, sb_=sb_, t3=t3: e.tensor_tensor(out=tmp3[t3], in0=sb_[:, :].rearrange("p (a b) -> p a b", a=4), in1=bsbc, op=ALU.add),
                 reads=[("ps", 6 + i % 2), "bsbc"], writes=[("tmp3", t3)])
            yv = YT[:, 0:4, i * 128:(i + 1) * 128]
            P.op("pool", lambda e, yv=yv, t3=t3: e.tensor_tensor(out=yv, in0=yv, in1=tmp3[t3], op=ALU.mult),
                 reads=[("tmp3", t3)] + [("YT", fa, g) for fa in range(4)], writes=[("YT", fa, g) for fa in range(4)])

        va_stages = [va_s1, va_s2, va_s3, va_s4]
        SGU_LAG = 3
        VA_LAGS = ((3, SGU_LAG + 1), (2, 2), (1, 1))

        drain_steps = []
        for n in range(7):
            cb = CB_ORDER[n]
            sl = n % 3
            wkeys = [("wst", sl, 0), ("wst", sl, 1)]
            if n + 2 < 7:
                load_wst(n + 2)
            if cb == 0:
                pass
            elif cb in (0, 2, 3, 4, 6):
                for f in (range(1) if cb in (3, 4) else range(4)):
                    for g in (range(1) if cb in (3, 4) else range(NG)):
                        bi = pc[0] % 4; pc[0] += 1
                        bk = pcb[bi]; bkey = ("pc", bi)
                        for kc in range(8):
                            P.op("pe", lambda e, bk=bk, sl=sl, kc=kc, f=f, g=g: e.matmul(
                                bk[:, :], lhsT=wst[sl][:, kc, f * 128:(f + 1) * 128], rhs=hT[:, kc, g * 512:(g + 1) * 512],
                                start=(kc == 0), stop=(kc == 7)),
                                reads=[wkeys[kc // 4], ("hT", kc, g)], writes=[bkey])
                        tok = slice(g * 512, (g + 1) * 512)
                        if cb == 0:
                            P.op("act", lambda e, bk=bk, f=f, tok=tok: e.activation(out=YT[:, f, tok], in_=bk[:, :], func=AF.Gelu_apprx_tanh),
                                 reads=[bkey], writes=[("YT", f, g)])
                        elif cb == 2:
                            ti = (f * NG + g) % 3
                            P.op("act", lambda e, bk=bk, ti=ti: e.activation(out=g32[ti], in_=bk[:, :], func=AF.Silu),
                                 reads=[bkey], writes=[("g32", ti)])
                            P.op("dve", lambda e, f=f, tok=tok, ti=ti: e.tensor_tensor(out=YT[:, f, tok], in0=YT[:, f, tok], in1=g32[ti], op=ALU.mult),
                                 reads=[("g32", ti), ("YT", f, g)], writes=[("YT", f, g)])
                        elif cb == 3:
                            P.op("act", lambda e, bk=bk, f=f, tok=tok: e.activation(out=qT[:, f, tok], in_=bk[:, :], func=AF.Identity, scale=0.125),
                                 reads=[bkey], writes=[("qT", f, g)])
                        elif cb == 4:
                            P.op("dve", lambda e, bk=bk, f=f, tok=tok: e.tensor_copy(out=kT[:, f, tok], in_=bk[:, :]),
                                 reads=[bkey], writes=[("kT", f, g)])
                        else:
                            P.op("act", lambda e, bk=bk, f=f, tok=tok: e.activation(out=YT[:, 4 + f, tok], in_=bk[:, :], func=AF.Silu),
                                 reads=[bkey], writes=[("YT", 4 + f, g)])
                        if drain_steps:
                            drain_steps.pop(0)()
            elif cb == 5:
                pass
            else:
                sl_vb = (n + 1) % 3
                wkeys_vb = [("wst", sl_vb, 0), ("wst", sl_vb, 1)]
                for i in range(NT):
                    g = i // 4
                    if cb == 1:
                        for st_no, lag in VA_LAGS:
                            ii = i - lag
                            if 0 <= ii < NT:
                                va_stages[st_no](ii)
                    bi = pc[0] % 4; pc[0] += 1
                    bk = pcb[bi]; bkey = ("pc", bi)
                    for kc in range(8):
                        P.op("pe", lambda e, bk=bk, sl=sl, kc=kc, i=i: e.matmul(
                            bk[:, :], lhsT=hT[:, kc, i * 128:(i + 1) * 128], rhs=wst[sl][:, kc, :],
                            start=(kc == 0), stop=(kc == 7)),
                            reads=[wkeys[kc // 4], ("hT", kc, g)], writes=[bkey])
                    va_bk[i] = (bk, bkey)
                    va_stages[0](i)
                    if i - SGU_LAG >= 0:
                        va_s3pe(i - SGU_LAG)
                    bi2 = pc[0] % 4; pc[0] += 1
                    bk2 = pcb[bi2]; bkey2 = ("pc", bi2)
                    for kc in range(8):
                        P.op("pe", lambda e, bk2=bk2, sl_vb=sl_vb, kc=kc, i=i: e.matmul(
                            bk2[:, :], lhsT=hT[:, kc, i * 128:(i + 1) * 128], rhs=wst[sl_vb][:, kc, :],
                            start=(kc == 0), stop=(kc == 7)),
                            reads=[wkeys_vb[kc // 4], ("hT", kc, g)], writes=[bkey2])
                    P.op("act", lambda e, bk2=bk2, i=i: e.activation(out=vb[:, i, :], in_=bk2[:, :], func=AF.Identity),
                         reads=[bkey2], writes=[("vb", i)])
                if cb == 1:
                    def mk_drain(step):
                        def f_():
                            for st_no, lag in VA_LAGS:
                                ii = step - lag
                                if 0 <= ii < NT:
                                    va_stages[st_no](ii)
                            if 0 <= step - SGU_LAG < NT:
                                va_s3pe(step - SGU_LAG)
                        return f_
                    drain_steps = [mk_drain(step) for step in range(NT, NT + SGU_LAG + 2)]
            if n == 0:
                ada_block(4)
                ada_block(5)
                P.fence()

        while drain_steps:
            drain_steps.pop(0)()
        P.fence()
        P.op("dve", lambda e: e.tensor_copy(out=negtri_r, in_=cst[:, 3, :]), reads=["cst"], writes=["negtri"])
        P.op("dve", lambda e: e.tensor_copy(out=negones_r, in_=cst[:, 4, :]), reads=["cst"], writes=["negones"])
        def late_loads():
            wout_v = wout_d.rearrange("(c p) n -> p c n", p=128)
            for hf in range(2):
                dma("pool", wout[:, hf * 4:(hf + 1) * 4, :], wout_v[:, hf * 4:(hf + 1) * 4, :], [], [("wout", hf)], ("wout", hf))
            dma("sp", lng, lng_d, [], ["lng"], "lng")
            dma("sp", lnb, lnb_d, [], ["lnb"], "lnb")

        def late_fold():
            for c in range(8):
                P.op("pool", lambda e, c=c: e.tensor_tensor(out=wout[:, c, :], in0=wout[:, c, :], in1=gate_bc, op=ALU.mult),
                     reads=[("wout", c // 4), ("gate_bc", 0), ("gate_bc", 1)], writes=[("wout", c // 4)])


        zf = [banks[i] for i in range(NZF)]
        SLQ, SLK = 5 % 3, 6 % 3

        def proj_ops(f, groups=None):
            ops = []
            if groups is None:
                groups = range(NG)
            for which, sl in (("q", SLQ), ("k", SLK)):
                for g in groups:
                    tok = slice(g * 512, (g + 1) * 512)
                    for kc in range(8):
                        ops.append(lambda sl=sl, kc=kc, f=f, g=g: P.op("pe", lambda e: e.matmul(
                            pjb[:, :], lhsT=wst[sl][:, kc, f * 128:(f + 1) * 128], rhs=hT[:, kc, g * 512:(g + 1) * 512],
                            start=(kc == 0), stop=(kc == 7)),
                            reads=[("wst", sl, kc // 4), ("hT", kc, g)], writes=["pjb"]))
                    if which == "q":
                        ops.append(lambda f=f, g=g, tok=tok: P.op("dve", lambda e: e.tensor_scalar(
                            out=qT[:, f, tok], in0=pjb[:, :], scalar1=0.125, scalar2=None, op0=ALU.mult),
                            reads=["pjb"], writes=[("qT", f, g)]))
                    else:
                        ops.append(lambda f=f, g=g, tok=tok: P.op("dve", lambda e: e.tensor_copy(out=kT[:, f, tok], in_=pjb[:, :]),
                                                                 reads=["pjb"], writes=[("kT", f, g)]))
            return ops

        pending = []
        npop = [0]
        accb = [banks[6], banks[6]]
        pjb = banks[7]
        zc = [0]; ac = [0]; lc = [0]; ec = [0]
        for hp in range(4):
            if hp == 1:
                late_loads()
            if hp == 2:
                late_fold()
            for J in range(NG):
                ai = 0
                if J == 0:
                    while pending:
                        pending.pop(0)()
                    npop[0] = 0
                    if hp == 0:
                        for g_ in range(1, NG):
                            pending = pending + proj_ops(0, groups=[g_])
                    if hp + 1 < 4:
                        pending = pending + proj_ops(hp + 1)
                elif hp == 0:
                    while npop[0] < 18 * J and pending:
                        pending.pop(0)(); npop[0] += 1
                acc = accb[ai]
                nblk = 4 * J + 4
                tiles = []
                for I in range(nblk - 1, -1, -1):
                    r = I - 4 * J
                    c0 = 128 * r if r > 0 else 0
                    tiles.append((I, r, c0, 512 - c0))
                n = len(tiles)
                for hl in range(2):
                    P.op("pool", lambda e, hl=hl: e.tensor_copy(out=Ssl[hl], in_=zeros512),
                         reads=["zerosf"], writes=[("S", hl)])
                info = {}

                def emit_z(pos):
                    I, r, c0, N = tiles[pos]
                    ks = slice(I * 128, (I + 1) * 128)
                    qs = slice(J * 512 + c0, (J + 1) * 512)
                    for hl in range(2):
                        pr = slice(hl * 64, (hl + 1) * 64)
                        zi = zc[0] % NZF; zc[0] += 1
                        info[(pos, hl)] = dict(zi=zi)
                        P.op("pe", lambda e, zi=zi, N=N, ks=ks, qs=qs, pr=pr, hp=hp: e.matmul(
                            zf[zi][:, 0:N], lhsT=kT[pr, hp, ks], rhs=qT[pr, hp, qs], start=True, stop=True),
                            reads=[("kT", hp, I // 4), ("qT", hp, J)], writes=[("zf", zi)])
                        if r >= 0:
                            P.op("pe", lambda e, zi=zi: e.matmul(
                                zf[zi][:, 0:128], lhsT=identb, rhs=negmask_b, start=False, stop=True, skip_group_check=True),
                                reads=["identb", "negmask_b"], writes=[("zf", zi)])

                def emit_el(pos):
                    I, r, c0, N = tiles[pos]
                    for hl in range(2):
                        d = info[(pos, hl)]
                        zi = d["zi"]
                        ei = ec[0] % NE; ec[0] += 1
                        d["ei"] = ei
                        P.op("act", lambda e, zi=zi, N=N, ei=ei: e.activation(out=Esl[ei][:, 0:N], in_=zf[zi][:, 0:N], func=AF.Exp),
                             reads=[("zf", zi)], writes=[("E", ei)])
                    for hl in range(2):
                        d = info[(pos, hl)]
                        ei = d["ei"]
                        li = lc[0] % NLR; lc[0] += 1
                        d["li"] = li
                        P.op("act", lambda e, N=N, ei=ei, li=li: e.activation(out=Lsl[li][:, 0:N], in_=Esl[ei][:, 0:N], func=AF.Ln, bias=1.0),
                             reads=[("E", ei)], writes=[("L", li)])

                def emit_fin(pos):
                    I, r, c0, N = tiles[pos]
                    top = (pos == 0)
                    for hl in range(2):
                        d = info[(pos, hl)]
                        zi, li = d["zi"], d["li"]
                        P.op("pe", lambda e, zi=zi, N=N, li=li, top=top: e.matmul(
                            zf[zi][:, 0:N], lhsT=negtri_r, rhs=Lsl[li][:, 0:N], start=False, stop=top, skip_group_check=True),
                            reads=[("L", li), "negtri"], writes=[("zf", zi)])
                        if not top:
                            P.op("pe", lambda e, zi=zi, N=N, c0=c0, hl=hl: e.matmul(
                                zf[zi][:, 0:N], lhsT=negones_r, rhs=Ssl[hl][:, c0:512], start=False, stop=True, skip_group_check=True),
                                reads=[("S", hl), "negones"], writes=[("zf", zi)])
                    for hl in range(2):
                        d = info[(pos, hl)]
                        zi, li = d["zi"], d["li"]
                        if pos < n - 1:
                            P.op("dve", lambda e, li=li, hl=hl, c0=c0, N=N: e.tensor_tensor(out=Ssl[hl][:, c0:512], in0=Ssl[hl][:, c0:512], in1=Lsl[li][:, 0:N], op=ALU.add),
                                 reads=[("S", hl), ("L", li)], writes=[("S", hl)])
                        asl = ac[0] % NA; ac[0] += 1
                        d["asl"] = asl
                        P.op("act", lambda e, zi=zi, N=N, asl=asl: e.activation(out=Asl[asl][:, 0:N], in_=zf[zi][:, 0:N], func=AF.Exp),
                             reads=[("zf", zi)], writes=[("A", asl)])

                def emit_av(pos):
                    I, r, c0, N = tiles[pos]
                    top = (pos == 0)
                    for hl in range(2):
                        asl = info[(pos, hl)]["asl"]
                        pr = slice(hl * 64, (hl + 1) * 64)
                        h = 2 * hp + hl
                        P.op("pe", lambda e, pr=pr, c0=c0, N=N, I=I, h=h, asl=asl, top=top, acc=acc: e.matmul(
                            acc[pr, c0:512], lhsT=vb[:, I, h * 64:(h + 1) * 64], rhs=Asl[asl][:, 0:N], start=top, stop=(I == 0), skip_group_check=True),
                            reads=[("vb", I), ("A", asl)], writes=[("acc", ai, hl)])

                emit_z(0)
                for p in range(n + 2):
                    if p < n:
                        emit_el(p)
                    if 0 <= p - 1 < n:
                        emit_fin(p - 1)
                    if p + 1 < n:
                        emit_z(p + 1)
                    if 0 <= p - 2 < n:
                        emit_av(p - 2)
                    for _ in range(3):
                        if pending:
                            pending.pop(0)(); npop[0] += 1
                tok = slice(J * 512, (J + 1) * 512)
                P.op("dve", lambda e, acc=acc, hp=hp, tok=tok: e.tensor_tensor(out=YT[:, 4 + hp, tok], in0=acc[:, :], in1=YT[:, 4 + hp, tok], op=ALU.mult),
                     reads=[("acc", ai, 0), ("acc", ai, 1), ("YT", 4 + hp, J)], writes=[("YT", 4 + hp, J), ("acc", ai, 0), ("acc", ai, 1)])

        P.fence()
        ob = [(banks[0], banks[1]), (banks[2], banks[3]), (banks[4], banks[5])]
        outs = []
        for i in range(2):
            dma("sp", xf[i], x_d[i * 128:(i + 1) * 128, :], [], [("xf", i)], ("xf", i))

        def f_s1(i):
            g = i // 4
            pi = i % 3
            q = i % 4
            xs = i % 2
            for half in range(2):
                bk = ob[pi][half]
                for c in range(8):
                    P.op("pe", lambda e, bk=bk, c=c, i=i, half=half: e.matmul(
                        bk[:, :], lhsT=YT[:, c, i * 128:(i + 1) * 128], rhs=wout[:, c, half * 512:(half + 1) * 512],
                        start=(c == 0), stop=(c == 7)),
                        reads=[("YT", c, g), ("wout", c // 4)], writes=[("ob", pi, half)])
                P.op("dve", lambda e, bk=bk, q=q, half=half, xs=xs: e.scalar_tensor_tensor(
                    out=rr[q][:, half * 512:(half + 1) * 512], in0=xf[xs][:, half * 512:(half + 1) * 512], scalar=ALPHA, in1=bk[:, :],
                    op0=ALU.mult, op1=ALU.add),
                    reads=[("ob", pi, half), ("xf", xs)], writes=[("rr", q, half)])
            if i + 2 < NT:
                dma("sp", xf[xs], x_d[(i + 2) * 128:(i + 3) * 128, :], [], [("xf", xs)], ("xf", xs))

        def f_s1a(i):
            q = i % 4
            P.op("act", lambda e, q=q: e.activation(out=junkF, in_=rr[q], func=AF.Identity, accum_out=banks[6][:, 8 * q:8 * q + 1]),
                 reads=[("rr", q, 0), ("rr", q, 1)], writes=["junkF", ("stF", q, 0)])
            P.op("act", lambda e, q=q: e.activation(out=junkF, in_=rr[q], func=AF.Square, accum_out=banks[7][:, 8 * q:8 * q + 1]),
                 reads=[("rr", q, 0), ("rr", q, 1)], writes=["junkF", ("stF", q, 1)])

        def f_s1b(i):
            pi = i % 4
            P.op("dve", lambda e, pi=pi: e.tensor_scalar(out=mvA[pi][:, 0:1], in0=banks[6][:, 8 * pi:8 * pi + 1], scalar1=1.0 / D, scalar2=None, op0=ALU.mult),
                 reads=[("stF", pi, 0)], writes=[("mvF", pi)])
            P.op("dve", lambda e, pi=pi: e.tensor_tensor(out=stA[pi][:, 3:4], in0=mvA[pi][:, 0:1], in1=mvA[pi][:, 0:1], op=ALU.mult),
                 reads=[("mvF", pi)], writes=[("msqF", pi)])
            P.op("dve", lambda e, pi=pi: e.scalar_tensor_tensor(out=mvA[pi][:, 1:2], in0=banks[7][:, 8 * pi:8 * pi + 1], scalar=1.0 / D, in1=stA[pi][:, 3:4],
                                                                 op0=ALU.mult, op1=ALU.subtract),
                 reads=[("stF", pi, 1), ("msqF", pi)], writes=[("varF", pi)])

        def f_s1c(i):
            pi = i % 4
            P.op("act", lambda e, pi=pi: e.activation(out=rsA[pi][:, 0:1], in_=mvA[pi][:, 1:2], func=AF.Ln, bias=epsc[:, 0:1]),
                 reads=[("varF", pi), "epsc"], writes=[("vepsF", pi)])
            P.op("act", lambda e, pi=pi: e.activation(out=rsA[pi][:, 1:2], in_=rsA[pi][:, 0:1], func=AF.Exp, scale=-0.5),
                 reads=[("vepsF", pi)], writes=[("rstdF", pi)])

        def f_s2(i):
            pi = i % 2
            q = i % 4
            P.op("dve", lambda e, q=q: e.scalar_tensor_tensor(out=rsA[q][:, 2:3], in0=mvA[q][:, 0:1], scalar=-1.0, in1=rsA[q][:, 1:2],
                                                                 op0=ALU.mult, op1=ALU.mult),
                 reads=[("mvF", q), ("rstdF", q)], writes=[("nbF", q)])
            if i % 2 == 0 or i >= NT - 3:
                P.op("dve", lambda e, pi=pi, q=q: e.tensor_scalar(out=xn[pi], in0=rr[q], scalar1=rsA[q][:, 1:2], scalar2=rsA[q][:, 2:3],
                                                                   op0=ALU.mult, op1=ALU.add),
                     reads=[("rr", q, 0), ("rr", q, 1), ("rstdF", q), ("nbF", q)], writes=[("xn", pi)])
            else:
                P.op("act", lambda e, pi=pi, q=q: e.activation(out=xn[pi], in_=rr[q], func=AF.Identity, scale=rsA[q][:, 1:2], bias=rsA[q][:, 2:3]),
                     reads=[("rr", q, 0), ("rr", q, 1), ("rstdF", q), ("nbF", q)], writes=[("xn", pi)])
            P.op("pool" if i < NT - 3 else "dve", lambda e, pi=pi: e.tensor_tensor(out=xn[pi], in0=xn[pi], in1=lng, op=ALU.mult),
                 reads=[("xn", pi), "lng"], writes=[("xn", pi)])

        def f_s3(i):
            pi = i % 2
            P.op("dve", lambda e, pi=pi: e.tensor_tensor(out=oo[pi], in0=xn[pi], in1=lnb, op=ALU.add),
                 reads=[("xn", pi), "lnb"], writes=[("oo", pi)])
            outs.append(dma("sp", y_d[i * 128:(i + 1) * 128, :], oo[pi], [("oo", pi)], [], ("oo", pi)))

        f_stages = [f_s1, f_s1b, f_s2, f_s3]
        NFS = 5
        for step in range(NT + NFS - 1):
            for fn, lag in ((f_s1, 0), (f_s1a, 0), (f_s1b, 1), (f_s1c, 1), (f_s3, 4), (f_s2, 2)):
                ii = step - lag
                if 0 <= ii < NT:
                    fn(ii)

        P.emit(final_waits={"sp": outs})
    return nc, P


_CACHE = {}


def _consts():
    idx = np.arange(128)
    ident = np.eye(128, dtype=np.float32)
    mask_ts = np.where(idx[None, :] > idx[:, None], 0.0, -30000.0).astype(np.float32)
    mask_sgu = (idx[None, :] >= idx[:, None]).astype(np.float32)
    negtri = -(idx[:, None] >= idx[None, :]).astype(np.float32)
    negones = -np.ones((128, 128), np.float32)
    return np.ascontiguousarray(np.concatenate([ident, mask_ts, mask_sgu, negtri, negones], axis=1))


def kernel(x, c, w_ada, b_ada, w_in, sgu_ln_g, sgu_ln_b, w_spatial, b_spatial, w_out, ln_g, ln_b):
    f = np.float32
    x = np.asarray(x, f); c = np.asarray(c, f)
    w_ada = np.ascontiguousarray(np.asarray(w_ada, f)[0]); b_ada = np.asarray(b_ada, f)[0]
    w_in = np.ascontiguousarray(np.asarray(w_in, f)[0]); w_out = np.ascontiguousarray(np.asarray(w_out, f)[0])
    sg = np.asarray(sgu_ln_g, f)[0]; sb = np.asarray(sgu_ln_b, f)[0]
    ws = np.asarray(w_spatial, f)[0]; bs = np.asarray(b_spatial, f)[0]
    lg = np.asarray(ln_g, f)[0]; lb = np.asarray(ln_b, f)[0]
    if "nc" not in _CACHE:
        _CACHE["nc"] = build_nc()[0]
    nc = _CACHE["nc"]
    shared = {
        "w_ada": w_ada,
        "b_ada_col": np.ascontiguousarray(b_ada.reshape(24, 128).T),
        "b_ada_gate": np.ascontiguousarray(b_ada[2048:3072].reshape(1, 1024)),
        "w_in": w_in, "w_out": w_out,
        "sgu_g_bc": np.ascontiguousarray(np.broadcast_to(sg[None, :], (128, 512))),
        "sgu_b_bc": np.ascontiguousarray(np.broadcast_to(sb[None, :], (128, 512))),
        "wsT": np.ascontiguousarray(ws.transpose(2, 0, 1).reshape(128, 8 * 128)),
        "bs_bc": np.ascontiguousarray(np.repeat(bs.reshape(4, 2, 1, 128), 64, axis=2).reshape(4, 128, 128).transpose(1, 0, 2).reshape(128, 512)),
        "lng_bc": np.ascontiguousarray(np.broadcast_to(lg[None, :], (128, 1024))),
        "lnb_bc": np.ascontiguousarray(np.broadcast_to(lb[None, :], (128, 1024))),
        "consts": _consts(),
    }
    in_maps = []
    for b in range(8):
        m = dict(shared)
        m["x"] = np.ascontiguousarray(x[b])
        m["c_col"] = np.ascontiguousarray(c[b].reshape(8, 128).T)
        in_maps.append(m)
    res = run_bass_kernel_spmd(nc, in_maps, core_ids=list(range(8)))
    return np.stack([np.asarray(r["y"], dtype=np.float32) for r in res.results], axis=0)
```

```python
import contextlib
import numpy as np
import concourse.bass as bass
import concourse.mybir as mybir
from concourse.bass_utils import run_bass_kernel_spmd

F32 = mybir.dt.float32
BF16 = mybir.dt.bfloat16
F32R = mybir.dt.float32r
AF = mybir.ActivationFunctionType
ALU = mybir.AluOpType

D = 1024
S = 2048
NT = S // 128
NG = S // 512
DIN = 3584
LN_EPS = 1e-5
ALPHA = 2.0 ** 0.25
ENGS = ["pe", "act", "dve", "pool", "sp"]


class Op:
    __slots__ = ("eng", "fn", "deps", "is_dma", "semkey", "signal", "count", "idx")


class Prog:
    def __init__(self, nc):
        self.nc = nc
        self.eng_ops = {e: [] for e in ENGS}
        self.last_writer = {}
        self.readers = {}
        self.n = 0
        self.pending_fence = {e: [] for e in ENGS}

    def fence(self):
        lastops = [self.eng_ops[e][-1] for e in ENGS if self.eng_ops[e]]
        for e in ENGS:
            self.pending_fence[e] = list(lastops)

    def op(self, eng, fn, reads=(), writes=(), dma=False, semkey=None):
        o = Op()
        o.eng = eng; o.fn = fn; o.is_dma = dma; o.semkey = semkey
        o.signal = False; o.count = None; o.idx = self.n; self.n += 1
        deps = {}
        for b in reads:
            w = self.last_writer.get(b)
            if w is not None:
                deps[w.idx] = w
        for b in writes:
            w = self.last_writer.get(b)
            if w is not None:
                deps[w.idx] = w
            lastr = {}
            for r in self.readers.get(b, ()):
                if r.eng == eng and not r.is_dma and not dma:
                    continue
                if r.is_dma:
                    deps[r.idx] = r
                else:
                    lastr[r.eng] = r
            for r in lastr.values():
                deps[r.idx] = r
        for d in self.pending_fence[eng]:
            deps[d.idx] = d
        self.pending_fence[eng] = []
        if eng == "pe":
            deps = {k: d for k, d in deps.items() if d.eng != "pe"}
        o.deps = list(deps.values())
        for d in o.deps:
            d.signal = True
        for b in reads:
            self.readers.setdefault(b, []).append(o)
        for b in writes:
            self.last_writer[b] = o
            self.readers[b] = []
        self.eng_ops[eng].append(o)
        return o

    def emit(self, final_waits):
        nc = self.nc
        dma_keys = []
        allops = sorted([o for e in ENGS for o in self.eng_ops[e]], key=lambda o: o.idx)
        for o in allops:
            if o.is_dma and o.semkey not in dma_keys:
                dma_keys.append(o.semkey)
        with contextlib.ExitStack() as st:
            esem = {e: st.enter_context(nc.semaphore("s_" + e)) for e in ENGS if e != "sp"}
            dsem = {k: st.enter_context(nc.semaphore("d%d" % i)) for i, k in enumerate(dma_keys)}
            dcount = {k: 0 for k in dma_keys}
            ecount = {e: 0 for e in ENGS}
            for o in allops:
                if o.is_dma:
                    dcount[o.semkey] += 16
                    o.count = dcount[o.semkey]
                elif o.signal:
                    ecount[o.eng] += 1
                    o.count = ecount[o.eng]
            self.stats = dict(ecount=ecount, nops={e: len(self.eng_ops[e]) for e in ENGS}, ndsem=len(dma_keys))
            block = st.enter_context(nc.Block())

            def run_engine(e, engobj):
                waited = {}
                for o in self.eng_ops[e]:
                    for d in o.deps:
                        if d.is_dma:
                            sem = dsem[d.semkey]; key = ("d", d.semkey)
                        else:
                            sem = esem[d.eng]; key = ("e", d.eng)
                        if waited.get(key, 0) >= d.count:
                            continue
                        engobj.wait_ge(sem, d.count)
                        waited[key] = d.count
                    inst = o.fn(engobj)
                    if o.is_dma:
                        inst.then_inc(dsem[o.semkey], 16)
                    elif o.signal:
                        inst.then_inc(esem[e], 1)
                for d in final_waits.get(e, ()):
                    engobj.wait_ge(dsem[d.semkey], d.count)

            @block.tensor
            def _(eng):
                run_engine("pe", eng)

            @block.scalar
            def _(eng):
                run_engine("act", eng)

            @block.vector
            def _(eng):
                run_engine("dve", eng)

            @block.gpsimd
            def _(eng):
                run_engine("pool", eng)

            @block.sync
            def _(eng):
                run_engine("sp", eng)


KB = 1024
SBUF_AVAIL = 212800
NLS = 32
NZF = 6


def build_nc():
    nc = bass.Bass("TRN2", target_bir_lowering=False)

    def din(name, shape):
        return nc.dram_tensor(name, list(shape), F32, kind="ExternalInput").ap()

    x_d = din("x", [S, D])
    ccol_d = din("c_col", [128, 8])
    wada_d = din("w_ada", [D, 3 * D])
    bacol_d = din("b_ada_col", [128, 24])
    bagate_d = din("b_ada_gate", [1, D])
    win_d = din("w_in", [D, DIN])
    wout_d = din("w_out", [D, D])
    sgug_d = din("sgu_g_bc", [128, 512])
    sgub_d = din("sgu_b_bc", [128, 512])
    wsT_d = din("wsT", [128, 8 * 128])
    bsbc_d = din("bs_bc", [128, 4 * 128])
    lng_d = din("lng_bc", [128, D])
    lnb_d = din("lnb_bc", [128, D])
    cst_d = din("consts", [128, 5 * 128])
    y_d = nc.dram_tensor("y", [S, D], F32, kind="ExternalOutput").ap()

    P = Prog(nc)

    class Arena:
        def __init__(self, t, nbytes, dt):
            self.t = t; self.n = nbytes; self.cur = 0; self.dt = dt

        def take(self, shape, dt=F32, off=None):
            esz = 2 if dt == BF16 else 4
            n = 1
            for s_ in shape[1:]:
                n *= s_
            nbytes = n * esz
            if off is None:
                off = self.cur
                self.cur = (off + nbytes + 63) // 64 * 64
            assert off % 4 == 0 and nbytes % 4 == 0 and off + nbytes <= self.n, (off, nbytes, self.n)
            ap = self.t[:, off // 4:(off + nbytes) // 4]
            if dt != self.dt:
                ap = ap.bitcast(dt)
            if len(shape) == 3:
                ap = ap.rearrange("p (a b) -> p a b", a=shape[1])
            return ap

    with contextlib.ExitStack() as st:
        BASE_BYTES = 100 * KB
        base_t = st.enter_context(nc.sbuf_tensor("base", [128, BASE_BYTES // 4], F32))
        banks = [st.enter_context(nc.psum_tensor("bank%d" % i, [128, 512], F32)) for i in range(8)]
        B = Arena(base_t, BASE_BYTES, F32)
        cst = B.take([128, 5, 128])
        identf = cst[:, 0, :]; mask_ts = cst[:, 1, :]; mask_sgu = cst[:, 2, :]
        onesf = B.take([128, 128])
        zerosf = B.take([128, 512])
        zeros512 = zerosf
        cc = B.take([128, 8]); silu_c = B.take([128, 8])
        bac = B.take([128, 24]); modraw = B.take([128, 16]); modT = B.take([128, 16]); scale1p = B.take([128, 8])
        stA = [B.take([128, 12]) for _ in range(4)]
        mvA = [B.take([128, 2]) for _ in range(4)]
        rsA = [B.take([128, 4]) for _ in range(4)]
        mhalf = B.take([128, 1])
        identb = B.take([128, 128], BF16)
        negmask_b = B.take([128, 128], BF16)
        epsc = B.take([128, 1])
        gate_bc = B.take([128, 1024])
        sgug = B.take([128, 512]); sgub = B.take([128, 512])
        wsT_bf = B.take([128, 8, 128], BF16)
        bsbc = B.take([128, 4, 128])
        qT = B.take([128, 4, S], BF16)
        kT = B.take([128, 4, S], BF16)
        vb = B.take([128, NT, 512], BF16)
        YT = B.take([128, 8, S], BF16)
        REG = SBUF_AVAIL - BASE_BYTES - 256
        with nc.sbuf_tensor("RA", [128, REG // 4], F32) as ra_t:
            RA = Arena(ra_t, REG, F32)
            wada_st = [RA.take([128, 8, 512], BF16, off=i * 8 * KB) for i in range(2)]
            silu_bc = RA.take([128, 8, 128], BF16, off=16 * KB)
            wsT_st = RA.take([128, 8, 128], off=18 * KB)
            bagate = gate_bc
            junk = [RA.take([128, 128], off=22 * KB + i * 512) for i in range(2)]
            g32 = [RA.take([128, 512], off=i * 2 * KB) for i in range(3)]
            tmp2 = [RA.take([128, 512], off=6 * KB + i * 2 * KB) for i in range(3)]
            tmp3 = [RA.take([128, 4, 128], off=12 * KB + i * 2 * KB) for i in range(2)]
            van = [RA.take([128, 512], BF16, off=16 * KB + i * KB) for i in range(3)]
            NXT = 8
            xt = [RA.take([128, 1024], off=31 * KB + i * 4 * KB) for i in range(7)]
            xt.append(RA.take([128, 1024], off=18 * KB))
            hT = RA.take([128, 8, S], BF16, off=59 * KB)
            wst = [RA.take([128, 8, 512], BF16, off=o * KB) for o in (91, 23, 99)]
        RE2_BYTES = 40 * KB
        re2_t = st.enter_context(nc.sbuf_tensor("RE2", [128, RE2_BYTES // 4], F32))
        RE2 = Arena(re2_t, RE2_BYTES, F32)
        NA = 6
        Asl = [RE2.take([128, 512], BF16) for i in range(NA)]
        NE = 4
        Esl = [RE2.take([128, 512]) for i in range(NE)]
        wout = RE2.take([128, 8, 1024], BF16)
        lng = RE2.take([128, 1024]); lnb = RE2.take([128, 1024])
        NLR = 6
        RER_WORDS = 2 * 128 + (2 + NLR) * 512
        assert RER_WORDS * 4 + RE2_BYTES <= REG, (RER_WORDS * 4, REG)
        with nc.sbuf_tensor("RER", [128, RER_WORDS], F32R) as rer_t:
            negtri_r = rer_t[:, 0:128]
            negones_r = rer_t[:, 128:256]
            Ssl = [rer_t[:, 256 + i * 512:256 + (i + 1) * 512] for i in range(2)]
            Lsl = [rer_t[:, 256 + (2 + i) * 512:256 + (3 + i) * 512] for i in range(NLR)]
        with nc.sbuf_tensor("RF", [128, 11 * 1024], F32) as rf_t:
            RF = Arena(rf_t, 44 * KB, F32)
            rr = [RF.take([128, 1024]) for i in range(4)]
            xn = [RF.take([128, 1024]) for i in range(2)]
            oo = [RF.take([128, 1024]) for i in range(2)]
            junkF = RF.take([128, 1024])
            xf = [RF.take([128, 1024]) for i in range(2)]

        def dma(q, out, in_, reads, writes, key):
            return P.op(q, lambda e: e.dma_start(out=out, in_=in_), reads=reads, writes=writes, dma=True, semkey=key)

        dma("sp", cst, cst_d.rearrange("p (a b) -> p a b", a=5), [], ["cst"], "cst")
        dma("sp", cc, ccol_d, [], ["cc"], "cc")
        dma("sp", bac, bacol_d, [], ["bac"], "bac")
        dma("sp", bagate[0:1, :], bagate_d, [], ["bagate"], "bagate")
        wada_v = wada_d.rearrange("(kc p) n -> p kc n", p=128)

        def load_wada(cb, after=()):
            slot = cb % 2
            for hf in range(2):
                dma("pool", wada_st[slot][:, hf * 4:(hf + 1) * 4, :], wada_v[:, hf * 4:(hf + 1) * 4, cb * 512:(cb + 1) * 512],
                    list(after), [("wada", slot, hf)], ("wada", slot, hf))

        load_wada(0)
        load_wada(1)
        dma("sp", wsT_st, wsT_d.rearrange("p (a b) -> p a b", a=8), [], ["wsT_st"], "wsT_st")
        dma("sp", bsbc, bsbc_d.rearrange("p (a b) -> p a b", a=4), [], ["bsbc"], "bsbc")
        dma("sp", sgug, sgug_d, [], ["sgug"], "sgug")
        dma("sp", sgub, sgub_d, [], ["sgub"], "sgub")
        win_v = win_d.rearrange("(kc p) n -> p kc n", p=128)
        CB_ORDER = [0, 2, 1, 5, 6, 3, 4]

        def load_wst(n, after=()):
            cb = CB_ORDER[n]
            sl = n % 3
            for hf in range(2):
                dma("pool", wst[sl][:, hf * 4:(hf + 1) * 4, :], win_v[:, hf * 4:(hf + 1) * 4, cb * 512:(cb + 1) * 512],
                    list(after), [("wst", sl, hf)], ("wst", sl, hf))

        P.op("act", lambda e: e.activation(out=silu_c, in_=cc, func=AF.Silu), reads=["cc"], writes=["silu_c"])
        P.op("dve", lambda e: e.memset(onesf, 1.0), writes=["onesf"])
        P.op("dve", lambda e: e.memset(mhalf, -0.5), writes=["mhalf"])
        P.op("dve", lambda e: e.tensor_copy(out=identb, in_=identf), reads=["cst"], writes=["identb"])
        P.op("dve", lambda e: e.tensor_copy(out=negmask_b, in_=mask_ts), reads=["cst"], writes=["negmask_b"])
        P.op("dve", lambda e: e.memset(epsc, LN_EPS), writes=["epsc"])
        P.op("dve", lambda e: e.memset(zerosf, 0.0), writes=["zerosf"])
        for kc in range(8):
            P.op("dve", lambda e, kc=kc: e.tensor_scalar(out=silu_bc[:, kc, :], in0=onesf, scalar1=silu_c[:, kc:kc + 1],
                                                         scalar2=None, op0=ALU.mult),
                 reads=["onesf", "silu_c"], writes=[("silu_bc", kc)])
        for gi in range(8):
            P.op("pool", lambda e, gi=gi: e.tensor_tensor(out=wsT_bf[:, gi, :], in0=wsT_st[:, gi, :], in1=mask_sgu, op=ALU.mult),
                 reads=["wsT_st", "cst"], writes=["wsT_bf"])

        def ada_block(cb):
            slot = cb % 2
            bi = cb % 3
            bk = banks[bi]
            for kc in range(8):
                stp = (kc == 7 and cb < 4)
                P.op("pe", lambda e, slot=slot, kc=kc, bk=bk, stp=stp: e.matmul(
                    bk[:, :], lhsT=silu_bc[:, kc, :], rhs=wada_st[slot][:, kc, :], start=(kc == 0), stop=stp),
                    reads=[("wada", slot, kc // 4), ("silu_bc", kc)], writes=[("pc", bi)])
            if cb < 4:
                for jj in range(4):
                    j = cb * 4 + jj
                    ji = j % 2
                    P.op("dve", lambda e, bk=bk, jj=jj, ji=ji: e.tensor_tensor(out=junk[ji], in0=bk[:, jj * 128:(jj + 1) * 128], in1=identf, op=ALU.mult),
                         reads=[("pc", bi), "cst"], writes=[("junk", ji)])
                    P.op("dve", lambda e, j=j, ji=ji: e.reduce_sum(out=modraw[:, j:j + 1], in_=junk[ji], axis=mybir.AxisListType.X),
                         reads=[("junk", ji)], writes=[("modraw", j)])
            else:
                gb = cb - 4
                P.op("pe", lambda e, gb=gb, bk=bk: e.matmul(
                    bk[:, :], lhsT=onesf[0:1, :], rhs=bagate[0:1, gb * 512:(gb + 1) * 512], start=False, stop=True),
                    reads=["onesf", "bagate"], writes=[("pc", bi)])
                P.op("act", lambda e, gb=gb, bk=bk: e.activation(out=gate_bc[:, gb * 512:(gb + 1) * 512], in_=bk[:, :], func=AF.Identity),
                     reads=[("pc", bi)], writes=[("gate_bc", gb)])
            if cb + 2 < 4:
                load_wada(cb + 2)
            if cb == 3:
                load_wst(0)
            if cb == 0:
                for i in range(NXT):
                    dma("sp", xt[i], x_d[i * 128:(i + 1) * 128, :], [], [("xt", i)] + (["wsT_st"] if i == 7 else []), ("xt", i))

        for cb in range(4):
            ada_block(cb)
        P.op("dve", lambda e: e.tensor_tensor(out=modT, in0=modraw, in1=bac[:, 0:16], op=ALU.add),
             reads=[("modraw", j) for j in range(16)] + ["bac"], writes=["modT"])
        P.op("dve", lambda e: e.tensor_scalar(out=scale1p, in0=modT[:, 8:16], scalar1=1.0, scalar2=None, op0=ALU.add),
             reads=["modT"], writes=["scale1p"])

        trb = [banks[3], banks[4], banks[6], banks[7]]
        trk = [3, 4, 6, 7]
        ev = 0
        pcb = [banks[0], banks[1], banks[2], banks[5]]
        pc = [0]

        def ua_proj(g):
            sl = 0
            for f in range(4):
                bi = pc[0] % 4; pc[0] += 1
                bk = pcb[bi]; bkey = ("pc", bi)
                for kc in range(8):
                    P.op("pe", lambda e, bk=bk, sl=sl, kc=kc, f=f, g=g: e.matmul(
                        bk[:, :], lhsT=wst[sl][:, kc, f * 128:(f + 1) * 128], rhs=hT[:, kc, g * 512:(g + 1) * 512],
                        start=(kc == 0), stop=(kc == 7)),
                        reads=[("wst", sl, kc // 4), ("hT", kc, g)], writes=[bkey])
                tok = slice(g * 512, (g + 1) * 512)
                P.op("act", lambda e, bk=bk, f=f, tok=tok: e.activation(out=YT[:, f, tok], in_=bk[:, :], func=AF.Gelu_apprx_tanh),
                     reads=[bkey], writes=[("YT", f, g)])

        for g in range(NG):
            for kc in range(8):
                b = (g * 8 + kc) % 4
                for tt in range(4):
                    xs = (g * 4 + tt) % NXT
                    P.op("pe", lambda e, b=b, tt=tt, kc=kc, xs=xs: e.transpose(trb[b][:, tt * 128:(tt + 1) * 128],
                                                                                xt[xs][:, kc * 128:(kc + 1) * 128], identf),
                         reads=[("xt", xs), "cst"], writes=[("ps", trk[b])])
                dst = hT[:, kc, g * 512:(g + 1) * 512]
                if ev % 2 == 0:
                    P.op("act", lambda e, b=b, kc=kc, dst=dst: e.activation(out=dst, in_=trb[b][:, :], func=AF.Identity,
                                                                            scale=scale1p[:, kc:kc + 1], bias=modT[:, kc:kc + 1]),
                         reads=[("ps", trk[b]), "scale1p", "modT"], writes=[("hT", kc, g)])
                else:
                    P.op("dve", lambda e, b=b, kc=kc, dst=dst: e.tensor_scalar(out=dst, in0=trb[b][:, :], scalar1=scale1p[:, kc:kc + 1],
                                                                               scalar2=modT[:, kc:kc + 1], op0=ALU.mult, op1=ALU.add),
                         reads=[("ps", trk[b]), "scale1p", "modT"], writes=[("hT", kc, g)])
                ev += 1
            for tt in range(4):
                i = g * 4 + tt + NXT
                if i < NT:
                    xs = i % NXT
                    dma("sp", xt[xs], x_d[i * 128:(i + 1) * 128, :], [], [("xt", xs)], ("xt", xs))
            if g >= 1:
                ua_proj(g - 1)
            if g == 2:
                load_wada(4, after=[("hT", 7, 2)])
                load_wada(5, after=[("hT", 7, 2)])
            if g == 3:
                load_wst(1, after=[("hT", 7, 3)])
        ua_proj(NG - 1)

        sgb = [banks[6], banks[7]]

        def sgu_tail(i):
            ti = i % 3
            g = i // 4
            sb_ = sgb[i % 2]
            for gi in range(8):
                fa, gl = gi // 2, gi % 2
                P.op("pe", lambda e, sb_=sb_, gi=gi, fa=fa, gl=gl, ti=ti: e.matmul(
                    sb_[gl * 64:(gl + 1) * 64, fa * 128:(fa + 1) * 128], lhsT=van[ti][:, gi * 64:(gi + 1) * 64], rhs=wsT_bf[:, gi, :],
                    start=True, stop=True),
                    reads=[("van", ti), "wsT_bf"], writes=[("ps", 6 + i % 2)])
            t3 = i % 2
            P.op("dve", lambda e, sb_=sb_, t3=t3: e.tensor_tensor(out=tmp3[t3], in0=sb_[:, :].rearrange("p (a b) -> p a b", a=4), in1=bsbc, op=ALU.add),
                 reads=[("ps", 6 + i % 2), "bsbc"], writes=[("tmp3", t3)])
            yv = YT[:, 0:4, i * 128:(i + 1) * 128]
            P.op("pool", lambda e, yv=yv, t3=t3: e.tensor_tensor(out=yv, in0=yv, in1=tmp3[t3], op=ALU.mult),
                 reads=[("tmp3", t3)] + [("YT", fa, g) for fa in range(4)], writes=[("YT", fa, g) for fa in range(4)])

        va_bk = {}

        def va_s1(i):
            ti = i % 3
            bk, bkey = va_bk[i]
            P.op("act", lambda e, bk=bk, ti=ti: e.activation(out=g32[ti], in_=bk[:, :], func=AF.Gelu_apprx_tanh),
                 reads=[bkey], writes=[("g32", ti)])
            P.op("dve", lambda e, ti=ti: e.bn_stats(out=stA[ti][:, 0:6], in_=g32[ti]),
                 reads=[("g32", ti)], writes=[("stA", ti)])
            P.op("dve", lambda e, ti=ti: e.bn_aggr(out=mvA[ti], in_=stA[ti][:, 0:6]),
                 reads=[("stA", ti)], writes=[("mvA", ti)])
            P.op("dve", lambda e, ti=ti: e.tensor_scalar(out=rsA[ti][:, 0:1], in0=mvA[ti][:, 1:2], scalar1=LN_EPS, scalar2=None, op0=ALU.add),
                 reads=[("mvA", ti)], writes=[("veps", ti)])
            P.op("pool", lambda e, ti=ti: e.tensor_tensor(out=rsA[ti][:, 1:2], in0=rsA[ti][:, 0:1], in1=mhalf, op=ALU.pow),
                 reads=[("veps", ti), "mhalf"], writes=[("rstd", ti)])

        def va_s2(i):
            ti = i % 3
            P.op("dve", lambda e, ti=ti: e.scalar_tensor_tensor(out=rsA[ti][:, 2:3], in0=mvA[ti][:, 0:1], scalar=-1.0, in1=rsA[ti][:, 1:2],
                                                                 op0=ALU.mult, op1=ALU.mult),
                 reads=[("mvA", ti), ("rstd", ti)], writes=[("nb", ti)])
            P.op("dve", lambda e, ti=ti: e.tensor_scalar(out=tmp2[ti], in0=g32[ti], scalar1=rsA[ti][:, 1:2], scalar2=rsA[ti][:, 2:3],
                                                          op0=ALU.mult, op1=ALU.add),
                 reads=[("g32", ti), ("rstd", ti), ("nb", ti)], writes=[("tmp2", ti)])
            P.op("pool", lambda e, ti=ti: e.tensor_tensor(out=tmp2[ti], in0=tmp2[ti], in1=sgug, op=ALU.mult),
                 reads=[("tmp2", ti), "sgug"], writes=[("tmp2", ti)])

        def va_s3(i):
            ti = i % 3
            P.op("dve", lambda e, ti=ti: e.tensor_tensor(out=van[ti], in0=tmp2[ti], in1=sgub, op=ALU.add),
                 reads=[("tmp2", ti), "sgub"], writes=[("van", ti)])

        def va_s3pe(i):
            ti = i % 3
            sb_ = sgb[i % 2]
            for gi in range(8):
                fa, gl = gi // 2, gi % 2
                P.op("pe", lambda e, sb_=sb_, gi=gi, fa=fa, gl=gl, ti=ti: e.matmul(
                    sb_[gl * 64:(gl + 1) * 64, fa * 128:(fa + 1) * 128], lhsT=van[ti][:, gi * 64:(gi + 1) * 64], rhs=wsT_bf[:, gi, :],
                    start=True, stop=True),
                    reads=[("van", ti), "wsT_bf"], writes=[("ps", 6 + i % 2)])

        def va_s4(i):
            g = i // 4
            sb_ = sgb[i % 2]
            t3 = i % 2
            P.op("dve", lambda e, sb_=sb_, t3=t3: e.tensor_tensor(out=tmp3[t3], in0=sb_[:, :].rearrange("p (a b) -> p a b", a=4), in1=bsbc, op=ALU.add),
                 reads=[("ps", 6 + i % 2), "bsbc"], writes=[("tmp3", t3)])
            yv = YT[:, 0:4, i * 128:(i + 1) * 128]
            P.op("pool", lambda e, yv=yv, t3=t3: e.tensor_tensor(out=yv, in0=yv, in1=tmp3[t3], op=ALU.mult),
                 reads=[("tmp3", t3)] + [("YT", fa, g) for fa in range(4)], writes=[("YT", fa, g) for fa in range(4)])

        va_stages = [va_s1, va_s2, va_s3, va_s4]
        SGU_LAG = 3
        VA_LAGS = ((3, SGU_LAG + 1), (2, 2), (1, 1))

        drain_steps = []
        for n in range(7):
            cb = CB_ORDER[n]
            sl = n % 3
            wkeys = [("wst", sl, 0), ("wst", sl, 1)]
            if n + 2 < 7:
                load_wst(n + 2)
            if cb == 0:
                pass
            elif cb in (0, 2, 3, 4, 6):
                for f in (range(1) if cb in (3, 4) else range(4)):
                    for g in (range(1) if cb in (3, 4) else range(NG)):
                        bi = pc[0] % 4; pc[0] += 1
                        bk = pcb[bi]; bkey = ("pc", bi)
                        for kc in range(8):
                            P.op("pe", lambda e, bk=bk, sl=sl, kc=kc, f=f, g=g: e.matmul(
                                bk[:, :], lhsT=wst[sl][:, kc, f * 128:(f + 1) * 128], rhs=hT[:, kc, g * 512:(g + 1) * 512],
                                start=(kc == 0), stop=(kc == 7)),
                                reads=[wkeys[kc // 4], ("hT", kc, g)], writes=[bkey])
                        tok = slice(g * 512, (g + 1) * 512)
                        if cb == 0:
                            P.op("act", lambda e, bk=bk, f=f, tok=tok: e.activation(out=YT[:, f, tok], in_=bk[:, :], func=AF.Gelu_apprx_tanh),
                                 reads=[bkey], writes=[("YT", f, g)])
                        elif cb == 2:
                            ti = (f * NG + g) % 3
                            P.op("act", lambda e, bk=bk, ti=ti: e.activation(out=g32[ti], in_=bk[:, :], func=AF.Silu),
                                 reads=[bkey], writes=[("g32", ti)])
                            P.op("dve", lambda e, f=f, tok=tok, ti=ti: e.tensor_tensor(out=YT[:, f, tok], in0=YT[:, f, tok], in1=g32[ti], op=ALU.mult),
                                 reads=[("g32", ti), ("YT", f, g)], writes=[("YT", f, g)])
                        elif cb == 3:
                            P.op("act", lambda e, bk=bk, f=f, tok=tok: e.activation(out=qT[:, f, tok], in_=bk[:, :], func=AF.Identity, scale=0.125),
                                 reads=[bkey], writes=[("qT", f, g)])
                        elif cb == 4:
                            P.op("dve", lambda e, bk=bk, f=f, tok=tok: e.tensor_copy(out=kT[:, f, tok], in_=bk[:, :]),
                                 reads=[bkey], writes=[("kT", f, g)])
                        else:
                            P.op("act", lambda e, bk=bk, f=f, tok=tok: e.activation(out=YT[:, 4 + f, tok], in_=bk[:, :], func=AF.Silu),
                                 reads=[bkey], writes=[("YT", 4 + f, g)])
                        if drain_steps and (f * NG + g) % 2 == 1:
                            drain_steps.pop(0)()
            elif cb == 5:
                pass
            else:
                sl_vb = (n + 1) % 3
                wkeys_vb = [("wst", sl_vb, 0), ("wst", sl_vb, 1)]
                for i in range(NT):
                    g = i // 4
                    if cb == 1:
                        for st_no, lag in VA_LAGS:
                            ii = i - lag
                            if 0 <= ii < NT:
                                va_stages[st_no](ii)
                    bi = pc[0] % 4; pc[0] += 1
                    bk = pcb[bi]; bkey = ("pc", bi)
                    for kc in range(8):
                        P.op("pe", lambda e, bk=bk, sl=sl, kc=kc, i=i: e.matmul(
                            bk[:, :], lhsT=hT[:, kc, i * 128:(i + 1) * 128], rhs=wst[sl][:, kc, :],
                            start=(kc == 0), stop=(kc == 7)),
                            reads=[wkeys[kc // 4], ("hT", kc, g)], writes=[bkey])
                    va_bk[i] = (bk, bkey)
                    va_stages[0](i)
                    if i - SGU_LAG >= 0:
                        va_s3pe(i - SGU_LAG)
                    bi2 = pc[0] % 4; pc[0] += 1
                    bk2 = pcb[bi2]; bkey2 = ("pc", bi2)
                    for kc in range(8):
                        P.op("pe", lambda e, bk2=bk2, sl_vb=sl_vb, kc=kc, i=i: e.matmul(
                            bk2[:, :], lhsT=hT[:, kc, i * 128:(i + 1) * 128], rhs=wst[sl_vb][:, kc, :],
                            start=(kc == 0), stop=(kc == 7)),
                            reads=[wkeys_vb[kc // 4], ("hT", kc, g)], writes=[bkey2])
                    P.op("act", lambda e, bk2=bk2, i=i: e.activation(out=vb[:, i, :], in_=bk2[:, :], func=AF.Identity),
                         reads=[bkey2], writes=[("vb", i)])
                if cb == 1:
                    def mk_drain(step):
                        def f_():
                            for st_no, lag in VA_LAGS:
                                ii = step - lag
                                if 0 <= ii < NT:
                                    va_stages[st_no](ii)
                            if 0 <= step - SGU_LAG < NT:
                                va_s3pe(step - SGU_LAG)
                        return f_
                    drain_steps = [mk_drain(step) for step in range(NT, NT + SGU_LAG + 2)]
            if n == 0:
                ada_block(4)
                ada_block(5)
                P.fence()

        while drain_steps:
            drain_steps.pop(0)()
        P.fence()
        P.op("dve", lambda e: e.tensor_copy(out=negtri_r, in_=cst[:, 3, :]), reads=["cst"], writes=["negtri"])
        P.op("dve", lambda e: e.tensor_copy(out=negones_r, in_=cst[:, 4, :]), reads=["cst"], writes=["negones"])
        def late_loads():
            wout_v = wout_d.rearrange("(c p) n -> p c n", p=128)
            for hf in range(2):
                dma("pool", wout[:, hf * 4:(hf + 1) * 4, :], wout_v[:, hf * 4:(hf + 1) * 4, :], [], [("wout", hf)], ("wout", hf))
            dma("sp", lng, lng_d, [], ["lng"], "lng")
            dma("sp", lnb, lnb_d, [], ["lnb"], "lnb")

        def late_fold():
            for c in range(8):
                P.op("pool", lambda e, c=c: e.tensor_tensor(out=wout[:, c, :], in0=wout[:, c, :], in1=gate_bc, op=ALU.mult),
                     reads=[("wout", c // 4), ("gate_bc", 0), ("gate_bc", 1)], writes=[("wout", c // 4)])


        zf = [banks[i] for i in range(NZF)]
        SLQ, SLK = 5 % 3, 6 % 3

        def proj_ops(f, groups=None):
            ops = []
            if groups is None:
                groups = range(NG)
            for which, sl in (("q", SLQ), ("k", SLK)):
                for g in groups:
                    tok = slice(g * 512, (g + 1) * 512)
                    for kc in range(8):
                        ops.append(lambda sl=sl, kc=kc, f=f, g=g: P.op("pe", lambda e: e.matmul(
                            pjb[:, :], lhsT=wst[sl][:, kc, f * 128:(f + 1) * 128], rhs=hT[:, kc, g * 512:(g + 1) * 512],
                            start=(kc == 0), stop=(kc == 7)),
                            reads=[("wst", sl, kc // 4), ("hT", kc, g)], writes=["pjb"]))
                    if which == "q":
                        ops.append(lambda f=f, g=g, tok=tok: P.op("dve", lambda e: e.tensor_scalar(
                            out=qT[:, f, tok], in0=pjb[:, :], scalar1=0.125, scalar2=None, op0=ALU.mult),
                            reads=["pjb"], writes=[("qT", f, g)]))
                    else:
                        ops.append(lambda f=f, g=g, tok=tok: P.op("dve", lambda e: e.tensor_copy(out=kT[:, f, tok], in_=pjb[:, :]),
                                                                 reads=["pjb"], writes=[("kT", f, g)]))
            return ops

        pending = []
        npop = [0]
        accb = [banks[6], banks[6]]
        pjb = banks[7]
        zc = [0]; ac = [0]; lc = [0]; ec = [0]
        for hp in range(4):
            if hp == 1:
                late_loads()
            if hp == 2:
                late_fold()
            for J in range(NG):
                ai = 0
                if J == 0:
                    while pending:
                        pending.pop(0)()
                    npop[0] = 0
                    if hp == 0:
                        for g_ in range(1, NG):
                            pending = pending + proj_ops(0, groups=[g_])
                    if hp + 1 < 4:
                        pending = pending + proj_ops(hp + 1)
                elif hp == 0:
                    while npop[0] < 18 * J and pending:
                        pending.pop(0)(); npop[0] += 1
                acc = accb[ai]
                nblk = 4 * J + 4
                tiles = []
                for I in range(nblk - 1, -1, -1):
                    r = I - 4 * J
                    c0 = 128 * r if r > 0 else 0
                    tiles.append((I, r, c0, 512 - c0))
                n = len(tiles)
                for hl in range(2):
                    P.op("pool", lambda e, hl=hl: e.tensor_copy(out=Ssl[hl], in_=zeros512),
                         reads=["zerosf"], writes=[("S", hl)])
                info = {}

                def emit_z(pos):
                    I, r, c0, N = tiles[pos]
                    ks = slice(I * 128, (I + 1) * 128)
                    qs = slice(J * 512 + c0, (J + 1) * 512)
                    for hl in range(2):
                        pr = slice(hl * 64, (hl + 1) * 64)
                        zi = zc[0] % NZF; zc[0] += 1
                        info[(pos, hl)] = dict(zi=zi)
                        P.op("pe", lambda e, zi=zi, N=N, ks=ks, qs=qs, pr=pr, hp=hp: e.matmul(
                            zf[zi][:, 0:N], lhsT=kT[pr, hp, ks], rhs=qT[pr, hp, qs], start=True, stop=True),
                            reads=[("kT", hp, I // 4), ("qT", hp, J)], writes=[("zf", zi)])
                        if r >= 0:
                            P.op("pe", lambda e, zi=zi: e.matmul(
                                zf[zi][:, 0:128], lhsT=identb, rhs=negmask_b, start=False, stop=True, skip_group_check=True),
                                reads=["identb", "negmask_b"], writes=[("zf", zi)])

                def emit_el(pos):
                    I, r, c0, N = tiles[pos]
                    for hl in range(2):
                        d = info[(pos, hl)]
                        zi = d["zi"]
                        ei = ec[0] % NE; ec[0] += 1
                        d["ei"] = ei
                        P.op("act", lambda e, zi=zi, N=N, ei=ei: e.activation(out=Esl[ei][:, 0:N], in_=zf[zi][:, 0:N], func=AF.Exp),
                             reads=[("zf", zi)], writes=[("E", ei)])
                    for hl in range(2):
                        d = info[(pos, hl)]
                        ei = d["ei"]
                        li = lc[0] % NLR; lc[0] += 1
                        d["li"] = li
                        P.op("act", lambda e, N=N, ei=ei, li=li: e.activation(out=Lsl[li][:, 0:N], in_=Esl[ei][:, 0:N], func=AF.Ln, bias=1.0),
                             reads=[("E", ei)], writes=[("L", li)])

                def emit_fin(pos):
                    I, r, c0, N = tiles[pos]
                    top = (pos == 0)
                    for hl in range(2):
                        d = info[(pos, hl)]
                        zi, li = d["zi"], d["li"]
                        P.op("pe", lambda e, zi=zi, N=N, li=li, top=top: e.matmul(
                            zf[zi][:, 0:N], lhsT=negtri_r, rhs=Lsl[li][:, 0:N], start=False, stop=top, skip_group_check=True),
                            reads=[("L", li), "negtri"], writes=[("zf", zi)])
                        if not top:
                            P.op("pe", lambda e, zi=zi, N=N, c0=c0, hl=hl: e.matmul(
                                zf[zi][:, 0:N], lhsT=negones_r, rhs=Ssl[hl][:, c0:512], start=False, stop=True, skip_group_check=True),
                                reads=[("S", hl), "negones"], writes=[("zf", zi)])
                    for hl in range(2):
                        d = info[(pos, hl)]
                        zi, li = d["zi"], d["li"]
                        if pos < n - 1:
                            P.op("dve", lambda e, li=li, hl=hl, c0=c0, N=N: e.tensor_tensor(out=Ssl[hl][:, c0:512], in0=Ssl[hl][:, c0:512], in1=Lsl[li][:, 0:N], op=ALU.add),
                                 reads=[("S", hl), ("L", li)], writes=[("S", hl)])
                        asl = ac[0] % NA; ac[0] += 1
                        d["asl"] = asl
                        P.op("act", lambda e, zi=zi, N=N, asl=asl: e.activation(out=Asl[asl][:, 0:N], in_=zf[zi][:, 0:N], func=AF.Exp),
                             reads=[("zf", zi)], writes=[("A", asl)])

                def emit_av(pos):
                    I, r, c0, N = tiles[pos]
                    top = (pos == 0)
                    for hl in range(2):
                        asl = info[(pos, hl)]["asl"]
                        pr = slice(hl * 64, (hl + 1) * 64)
                        h = 2 * hp + hl
                        P.op("pe", lambda e, pr=pr, c0=c0, N=N, I=I, h=h, asl=asl, top=top, acc=acc: e.matmul(
                            acc[pr, c0:512], lhsT=vb[:, I, h * 64:(h + 1) * 64], rhs=Asl[asl][:, 0:N], start=top, stop=(I == 0), skip_group_check=True),
                            reads=[("vb", I), ("A", asl)], writes=[("acc", ai, hl)])

                emit_z(0)
                for p in range(n + 2):
                    if p < n:
                        emit_el(p)
                    if 0 <= p - 1 < n:
                        emit_fin(p - 1)
                    if p + 1 < n:
                        emit_z(p + 1)
                    if 0 <= p - 2 < n:
                        emit_av(p - 2)
                    for _ in range(3 if hp == 0 else 2):
                        if pending:
                            pending.pop(0)(); npop[0] += 1
                tok = slice(J * 512, (J + 1) * 512)
                P.op("dve", lambda e, acc=acc, hp=hp, tok=tok: e.tensor_tensor(out=YT[:, 4 + hp, tok], in0=acc[:, :], in1=YT[:, 4 + hp, tok], op=ALU.mult),
                     reads=[("acc", ai, 0), ("acc", ai, 1), ("YT", 4 + hp, J)], writes=[("YT", 4 + hp, J), ("acc", ai, 0), ("acc", ai, 1)])

        P.fence()
        ob = [(banks[0], banks[1]), (banks[2], banks[3]), (banks[4], banks[5])]
        outs = []
        for i in range(2):
            dma("sp", xf[i], x_d[i * 128:(i + 1) * 128, :], [], [("xf", i)], ("xf", i))

        def f_s1(i):
            g = i // 4
            pi = i % 3
            q = i % 4
            xs = i % 2
            for half in range(2):
                bk = ob[pi][half]
                for c in range(8):
                    P.op("pe", lambda e, bk=bk, c=c, i=i, half=half: e.matmul(
                        bk[:, :], lhsT=YT[:, c, i * 128:(i + 1) * 128], rhs=wout[:, c, half * 512:(half + 1) * 512],
                        start=(c == 0), stop=(c == 7)),
                        reads=[("YT", c, g), ("wout", c // 4)], writes=[("ob", pi, half)])
                P.op("dve", lambda e, bk=bk, q=q, half=half, xs=xs: e.scalar_tensor_tensor(
                    out=rr[q][:, half * 512:(half + 1) * 512], in0=xf[xs][:, half * 512:(half + 1) * 512], scalar=ALPHA, in1=bk[:, :],
                    op0=ALU.mult, op1=ALU.add),
                    reads=[("ob", pi, half), ("xf", xs)], writes=[("rr", q, half)])
            if i + 2 < NT:
                dma("sp", xf[xs], x_d[(i + 2) * 128:(i + 3) * 128, :], [], [("xf", xs)], ("xf", xs))

        def f_s1a(i):
            q = i % 4
            P.op("act", lambda e, q=q: e.activation(out=junkF, in_=rr[q], func=AF.Identity, accum_out=banks[6][:, 8 * q:8 * q + 1]),
                 reads=[("rr", q, 0), ("rr", q, 1)], writes=["junkF", ("stF", q, 0)])
            P.op("act", lambda e, q=q: e.activation(out=junkF, in_=rr[q], func=AF.Square, accum_out=banks[7][:, 8 * q:8 * q + 1]),
                 reads=[("rr", q, 0), ("rr", q, 1)], writes=["junkF", ("stF", q, 1)])

        def f_s1b(i):
            pi = i % 4
            P.op("dve", lambda e, pi=pi: e.tensor_scalar(out=mvA[pi][:, 0:1], in0=banks[6][:, 8 * pi:8 * pi + 1], scalar1=1.0 / D, scalar2=None, op0=ALU.mult),
                 reads=[("stF", pi, 0)], writes=[("mvF", pi)])
            P.op("dve", lambda e, pi=pi: e.tensor_tensor(out=stA[pi][:, 3:4], in0=mvA[pi][:, 0:1], in1=mvA[pi][:, 0:1], op=ALU.mult),
                 reads=[("mvF", pi)], writes=[("msqF", pi)])
            P.op("dve", lambda e, pi=pi: e.scalar_tensor_tensor(out=mvA[pi][:, 1:2], in0=banks[7][:, 8 * pi:8 * pi + 1], scalar=1.0 / D, in1=stA[pi][:, 3:4],
                                                                 op0=ALU.mult, op1=ALU.subtract),
                 reads=[("stF", pi, 1), ("msqF", pi)], writes=[("varF", pi)])

        def f_s1c(i):
            pi = i % 4
            P.op("act", lambda e, pi=pi: e.activation(out=rsA[pi][:, 0:1], in_=mvA[pi][:, 1:2], func=AF.Ln, bias=epsc[:, 0:1]),
                 reads=[("varF", pi), "epsc"], writes=[("vepsF", pi)])
            P.op("act", lambda e, pi=pi: e.activation(out=rsA[pi][:, 1:2], in_=rsA[pi][:, 0:1], func=AF.Exp, scale=-0.5),
                 reads=[("vepsF", pi)], writes=[("rstdF", pi)])

        def f_s2(i):
            pi = i % 2
            q = i % 4
            P.op("dve", lambda e, q=q: e.scalar_tensor_tensor(out=rsA[q][:, 2:3], in0=mvA[q][:, 0:1], scalar=-1.0, in1=rsA[q][:, 1:2],
                                                                 op0=ALU.mult, op1=ALU.mult),
                 reads=[("mvF", q), ("rstdF", q)], writes=[("nbF", q)])
            if i % 2 == 0 or i >= NT - 3:
                P.op("dve", lambda e, pi=pi, q=q: e.tensor_scalar(out=xn[pi], in0=rr[q], scalar1=rsA[q][:, 1:2], scalar2=rsA[q][:, 2:3],
                                                                   op0=ALU.mult, op1=ALU.add),
                     reads=[("rr", q, 0), ("rr", q, 1), ("rstdF", q), ("nbF", q)], writes=[("xn", pi)])
            else:
                P.op("act", lambda e, pi=pi, q=q: e.activation(out=xn[pi], in_=rr[q], func=AF.Identity, scale=rsA[q][:, 1:2], bias=rsA[q][:, 2:3]),
                     reads=[("rr", q, 0), ("rr", q, 1), ("rstdF", q), ("nbF", q)], writes=[("xn", pi)])
            P.op("pool" if i < NT - 3 else "dve", lambda e, pi=pi: e.tensor_tensor(out=xn[pi], in0=xn[pi], in1=lng, op=ALU.mult),
                 reads=[("xn", pi), "lng"], writes=[("xn", pi)])

        def f_s3(i):
            pi = i % 2
            P.op("dve", lambda e, pi=pi: e.tensor_tensor(out=oo[pi], in0=xn[pi], in1=lnb, op=ALU.add),
                 reads=[("xn", pi), "lnb"], writes=[("oo", pi)])
            outs.append(dma("sp", y_d[i * 128:(i + 1) * 128, :], oo[pi], [("oo", pi)], [], ("oo", pi)))

        f_stages = [f_s1, f_s1b, f_s2, f_s3]
        NFS = 5
        for step in range(NT + NFS - 1):
            for fn, lag in ((f_s1, 0), (f_s1a, 0), (f_s1b, 1), (f_s1c, 1), (f_s3, 4), (f_s2, 2)):
                ii = step - lag
                if 0 <= ii < NT:
                    fn(ii)

        P.emit(final_waits={"sp": outs})
    return nc, P


_CACHE = {}


def _consts():
    idx = np.arange(128)
    ident = np.eye(128, dtype=np.float32)
    mask_ts = np.where(idx[None, :] > idx[:, None], 0.0, -30000.0).astype(np.float32)
    mask_sgu = (idx[None, :] >= idx[:, None]).astype(np.float32)
    negtri = -(idx[:, None] >= idx[None, :]).astype(np.float32)
    negones = -np.ones((128, 128), np.float32)
    return np.ascontiguousarray(np.concatenate([ident, mask_ts, mask_sgu, negtri, negones], axis=1))


def kernel(x, c, w_ada, b_ada, w_in, sgu_ln_g, sgu_ln_b, w_spatial, b_spatial, w_out, ln_g, ln_b):
    f = np.float32
    x = np.asarray(x, f); c = np.asarray(c, f)
    w_ada = np.ascontiguousarray(np.asarray(w_ada, f)[0]); b_ada = np.asarray(b_ada, f)[0]
    w_in = np.ascontiguousarray(np.asarray(w_in, f)[0]); w_out = np.ascontiguousarray(np.asarray(w_out, f)[0])
    sg = np.asarray(sgu_ln_g, f)[0]; sb = np.asarray(sgu_ln_b, f)[0]
    ws = np.asarray(w_spatial, f)[0]; bs = np.asarray(b_spatial, f)[0]
    lg = np.asarray(ln_g, f)[0]; lb = np.asarray(ln_b, f)[0]
    if "nc" not in _CACHE:
        _CACHE["nc"] = build_nc()[0]
    nc = _CACHE["nc"]
    shared = {
        "w_ada": w_ada,
        "b_ada_col": np.ascontiguousarray(b_ada.reshape(24, 128).T),
        "b_ada_gate": np.ascontiguousarray(b_ada[2048:3072].reshape(1, 1024)),
        "w_in": w_in, "w_out": w_out,
        "sgu_g_bc": np.ascontiguousarray(np.broadcast_to(sg[None, :], (128, 512))),
        "sgu_b_bc": np.ascontiguousarray(np.broadcast_to(sb[None, :], (128, 512))),
        "wsT": np.ascontiguousarray(ws.transpose(2, 0, 1).reshape(128, 8 * 128)),
        "bs_bc": np.ascontiguousarray(np.repeat(bs.reshape(4, 2, 1, 128), 64, axis=2).reshape(4, 128, 128).transpose(1, 0, 2).reshape(128, 512)),
        "lng_bc": np.ascontiguousarray(np.broadcast_to(lg[None, :], (128, 1024))),
        "lnb_bc": np.ascontiguousarray(np.broadcast_to(lb[None, :], (128, 1024))),
        "consts": _consts(),
    }
    in_maps = []
    for b in range(8):
        m = dict(shared)
        m["x"] = np.ascontiguousarray(x[b])
        m["c_col"] = np.ascontiguousarray(c[b].reshape(8, 128).T)
        in_maps.append(m)
    res = run_bass_kernel_spmd(nc, in_maps, core_ids=list(range(8)))
    return np.stack([np.asarray(r["y"], dtype=np.float32) for r in res.results], axis=0)
```
